# Optimizing a Trainium2 kernel written in Bass

```python
import jax, jax.numpy as jnp
from jax import lax
import numpy as np

D_MODEL = 1024
BATCH = 4
SEQ = 4096
DEPTH = 4

CONV_WIDTH = D_MODEL
CONV_K = 3
N_Q_HEADS = 16
N_KV_HEADS = 4
HEAD_DIM = 64
ATTN_WIDTH = N_Q_HEADS * HEAD_DIM
KV_WIDTH = N_KV_HEADS * HEAD_DIM
WINDOW = 128
BLOCK = 128
N_BRANCHES = 2
EPS = 1e-6
NEG_INF = -1e30

IN_SIZES = (CONV_WIDTH, CONV_WIDTH, CONV_WIDTH, CONV_WIDTH,
            ATTN_WIDTH, KV_WIDTH, KV_WIDTH, ATTN_WIDTH,
            N_BRANCHES * D_MODEL)
IN_COLS = sum(IN_SIZES)
SPLIT_POINTS = tuple(int(c) for c in np.cumsum(IN_SIZES)[:-1])

kernel_name = "hybrid_shortconv_swa_sink_gated_block"


def rms_norm(x, g):
    xf = x.astype(jnp.float32)
    y = xf * lax.rsqrt(jnp.mean(xf * xf, axis=-1, keepdims=True) + EPS)
    return (y * g.astype(jnp.float32)).astype(x.dtype)


def causal_depthwise_conv(u, w):
    s = u.shape[1]
    up = jnp.pad(u, ((0, 0), (CONV_K - 1, 0), (0, 0)))
    y = up[:, 0:s] * w[0]
    for k in range(1, CONV_K):
        y = y + up[:, k:k + s] * w[k]
    return y


def sliding_window_attention(q, k, v, sinks):
    b, s = q.shape[0], q.shape[1]
    nb = s // BLOCK
    g = N_Q_HEADS // N_KV_HEADS
    qb = q.reshape(b, nb, BLOCK, N_KV_HEADS, g, HEAD_DIM)

    def band(t):
        tb = t.reshape(b, nb, BLOCK, N_KV_HEADS, HEAD_DIM)
        prev = jnp.pad(tb[:, :-1], ((0, 0), (1, 0), (0, 0), (0, 0), (0, 0)))
        return jnp.concatenate([prev, tb], axis=2)

    kb, vb = band(k), band(v)
    scale = HEAD_DIM ** -0.5
    scores = jnp.einsum('bnqhgd,bnkhd->bnhgqk', qb.astype(jnp.float32),
                        kb.astype(jnp.float32)) * scale
    blk = jnp.arange(nb)[:, None, None]
    q_pos = blk * BLOCK + jnp.arange(BLOCK)[None, :, None]
    k_pos = (blk - 1) * BLOCK + jnp.arange(2 * BLOCK)[None, None, :]
    diff = q_pos - k_pos
    valid = (diff >= 0) & (diff < WINDOW) & (k_pos >= 0)
    scores = jnp.where(valid[None, :, None, None], scores, NEG_INF)
    sink = sinks.astype(jnp.float32).reshape(N_KV_HEADS, g)[None, None, :, :, None, None]
    m = jnp.maximum(jnp.max(scores, axis=-1, keepdims=True), sink)
    p = jnp.exp(scores - m)
    p = p / (jnp.sum(p, axis=-1, keepdims=True) + jnp.exp(sink - m))
    out = jnp.einsum('bnhgqk,bnkhd->bnqhgd', p.astype(v.dtype), vb)
    return out.reshape(b, s, ATTN_WIDTH)


def hybrid_layer(x, norm_g, w_in, conv_w, q_norm_g, k_norm_g, sinks,
                 w_conv_out, w_attn_out, gate_b, w_out):
    b, s, _ = x.shape
    h = rms_norm(x, norm_g)
    u = jnp.einsum('bsd,dc->bsc', h, w_in)
    v_c, b_c, c_c, z_c, q, k, v, z_a, gate_logits = jnp.split(u, SPLIT_POINTS, axis=-1)

    y_c = b_c * causal_depthwise_conv(c_c * v_c, conv_w)
    y_c = y_c * jax.nn.silu(z_c)
    y_a = jnp.einsum('bsc,cd->bsd', y_c, w_conv_out)

    q = rms_norm(q.reshape(b, s, N_Q_HEADS, HEAD_DIM), q_norm_g)
    k = rms_norm(k.reshape(b, s, N_KV_HEADS, HEAD_DIM), k_norm_g)
    v = v.reshape(b, s, N_KV_HEADS, HEAD_DIM)
    o = sliding_window_attention(q, k, v, sinks) * jax.nn.silu(z_a)
    y_b = jnp.einsum('bsc,cd->bsd', o, w_attn_out)

    gates = jax.nn.sigmoid(gate_logits + gate_b)
    g_a, g_b = jnp.split(gates, N_BRANCHES, axis=-1)
    merged = g_a * y_a + g_b * y_b
    return x + jnp.einsum('bsd,de->bse', merged, w_out)


def setup_inputs(seed: int = 0) -> dict:
    key = jax.random.key(seed)
    ks = jax.random.split(key, 12)
    f32 = jnp.float32
    x = jax.random.normal(ks[0], (BATCH, SEQ, D_MODEL), f32)
    norm_g = 1.0 + 0.05 * jax.random.normal(ks[1], (DEPTH, D_MODEL), f32)
    w_in = jax.random.normal(ks[2], (DEPTH, D_MODEL, IN_COLS), f32) * D_MODEL ** -0.5
    conv_w = jax.random.normal(ks[3], (DEPTH, CONV_K, CONV_WIDTH), f32) * CONV_K ** -0.5
    q_norm_g = 1.0 + 0.05 * jax.random.normal(ks[4], (DEPTH, HEAD_DIM), f32)
    k_norm_g = 1.0 + 0.05 * jax.random.normal(ks[5], (DEPTH, HEAD_DIM), f32)
    sinks = 0.5 * jax.random.normal(ks[6], (DEPTH, N_Q_HEADS), f32)
    w_conv_out = jax.random.normal(ks[7], (DEPTH, CONV_WIDTH, D_MODEL), f32) * CONV_WIDTH ** -0.5
    w_attn_out = jax.random.normal(ks[8], (DEPTH, ATTN_WIDTH, D_MODEL), f32) * ATTN_WIDTH ** -0.5
    gate_b = 0.02 * jax.random.normal(ks[9], (DEPTH, N_BRANCHES * D_MODEL), f32)
    w_out = jax.random.normal(ks[10], (DEPTH, D_MODEL, D_MODEL), f32) * D_MODEL ** -0.5
    return {"x": x, "norm_g": norm_g, "w_in": w_in, "conv_w": conv_w,
            "q_norm_g": q_norm_g, "k_norm_g": k_norm_g, "sinks": sinks,
            "w_conv_out": w_conv_out, "w_attn_out": w_attn_out,
            "gate_b": gate_b, "w_out": w_out}


def reference(x, norm_g, w_in, conv_w, q_norm_g, k_norm_g, sinks,
              w_conv_out, w_attn_out, gate_b, w_out):
    for l in range(DEPTH):
        x = hybrid_layer(x, norm_g[l], w_in[l], conv_w[l], q_norm_g[l], k_norm_g[l],
                         sinks[l], w_conv_out[l], w_attn_out[l], gate_b[l], w_out[l])
    return x
```

```python
import numpy as np
from contextlib import ExitStack

import concourse.bass as bass
import concourse.mybir as mybir
from concourse.bass_utils import run_bass_kernel_spmd

F32 = mybir.dt.float32
BF16 = mybir.dt.bfloat16
AF = mybir.ActivationFunctionType
ALU = mybir.AluOpType

D = 1024
NCOL = 8704
C_V, C_B, C_C, C_Z = 0, 1024, 2048, 3072
C_Q, C_K, C_VV, C_ZA, C_G = 4096, 5120, 5376, 5632, 6656
EPS = 1e-6
SEMCHUNK = 16384


class Op:
    __slots__ = ("eng", "fn", "waits", "signal", "seq", "dma_key", "dma_cnt",
                 "dma_n", "clock", "sigidx", "dma_inc")


class Sched:
    ENGS = ("pe", "act", "dve", "pool", "sp")

    def __init__(self):
        self.ops = {e: [] for e in self.ENGS}
        self.know = {e: {} for e in self.ENGS}
        self.lastw = {}
        self.readers = {}
        self.dmacnt = {}

    def add(self, eng, fn, reads=(), writes=(), dma=None, dma_n=1, dma_inc=16):
        op = Op()
        op.dma_inc = dma_inc
        op.eng, op.fn, op.waits, op.signal = eng, fn, [], False
        op.dma_key, op.dma_n, op.sigidx = dma, dma_n, None
        know = self.know[eng]
        deps = []
        for r in reads:
            w = self.lastw.get(r)
            if w is not None:
                deps.append((w, True))
        for w_ in writes:
            lw = self.lastw.get(w_)
            if lw is not None:
                deps.append((lw, False))
            rd = self.readers.get(w_)
            if rd:
                for o in rd.values():
                    deps.append((o, False))
        for d, raw in deps:
            if d.dma_key is not None:
                key, val = ("dma", d.dma_key), d.dma_cnt
            else:
                if d.eng == eng and eng == "pe":
                    continue
                key, val = ("eng", d.eng), d.seq
            if know.get(key, -1) >= val:
                continue
            op.waits.append(d)
            d.signal = True
            for k2, v2 in d.clock.items():
                if know.get(k2, -1) < v2:
                    know[k2] = v2
        op.seq = len(self.ops[eng])
        op.clock = dict(know)
        if dma is not None:
            c = self.dmacnt.get(dma, 0) + dma_n
            self.dmacnt[dma] = c
            op.dma_cnt = c
            op.clock[("dma", dma)] = c
            rkey = ("dma", dma, c)
        else:
            op.dma_cnt = None
            op.clock[("eng", eng)] = op.seq
            rkey = eng
        for r in reads:
            self.readers.setdefault(r, {})[rkey] = op
        for w_ in writes:
            self.lastw[w_] = op
            self.readers[w_] = {}
        self.ops[eng].append(op)
        return op

    def emit(self, nc, es):
        esems = {}
        for e in self.ENGS:
            n = 0
            for op in self.ops[e]:
                if op.signal and op.dma_key is None:
                    op.sigidx = n
                    n += 1
            esems[e] = [es.enter_context(nc.semaphore(f"s_{e}_{i}"))
                        for i in range((n + SEMCHUNK - 1) // SEMCHUNK)]
        dsems = {k: es.enter_context(nc.semaphore("d_" + "_".join(str(x) for x in (k if isinstance(k, tuple) else (k,)))))
                 for k in self.dmacnt}

        def sem_of(d):
            if d.dma_key is not None:
                return dsems[d.dma_key], d.dma_inc * d.dma_cnt
            return esems[d.eng][d.sigidx // SEMCHUNK], d.sigidx % SEMCHUNK + 1

        def body(engobj, e):
            for op in self.ops[e]:
                waits = list(op.waits)
                attach = None
                if waits and op.fn is not None and op.dma_key is None:
                    attach = waits.pop()
                for d in waits:
                    sem, val = sem_of(d)
                    engobj.wait_ge(sem, val)
                if op.fn is None:
                    continue
                res = op.fn(engobj)
                if attach is not None:
                    sem, val = sem_of(attach)
                    res._wait_ge(sem, val)
                if op.dma_key is not None:
                    lst = res if isinstance(res, (list, tuple)) else [res]
                    assert len(lst) == op.dma_n
                    for i_ in lst:
                        i_.then_inc(dsems[op.dma_key], op.dma_inc)
                elif op.signal:
                    sem, _ = sem_of(op)
                    res.then_inc(sem, 1)

        block = es.enter_context(nc.Block())

        @block.tensor
        def _(t):
            body(t, "pe")

        @block.scalar
        def _(s):
            body(s, "act")

        @block.vector
        def _(v):
            body(v, "dve")

        @block.gpsimd
        def _(g):
            body(g, "pool")

        @block.sync
        def _(sy):
            body(sy, "sp")


def param_offsets(NL):
    o = {}
    c = 0
    for name, n in (("ng", NL * 8), ("gb", NL * 16), ("cw", NL * 24), ("qg", NL),
                    ("kg", NL), ("sk", NL * 16), ("fr", 1)):
        o[name] = c
        c += n
    o["_n"] = c
    return o


def pack_params(norm_g, gate_b, conv_w, q_norm_g, k_norm_g, sinks, fr):
    NL = norm_g.shape[0]
    o = param_offsets(NL)
    P = np.zeros((128, o["_n"]), np.float32)
    P[:, o["ng"]:o["ng"] + NL * 8] = norm_g.reshape(NL, 8, 128).transpose(2, 0, 1).reshape(128, NL * 8)
    P[:, o["gb"]:o["gb"] + NL * 16] = gate_b.reshape(NL, 16, 128).transpose(2, 0, 1).reshape(128, NL * 16)
    P[:, o["cw"]:o["cw"] + NL * 24] = conv_w.reshape(NL, 3, 8, 128).transpose(3, 0, 1, 2).reshape(128, NL * 24)
    P[:, o["qg"]:o["qg"] + NL] = np.concatenate([q_norm_g, q_norm_g], axis=1).T
    P[:, o["kg"]:o["kg"] + NL] = np.concatenate([k_norm_g, k_norm_g], axis=1).T
    P[:, o["sk"]:o["sk"] + NL * 16] = np.broadcast_to(sinks.reshape(1, NL * 16), (128, NL * 16))
    P[:, o["fr"]] = fr
    return P


def build(NL, NBLK, FR, OUT0, layer_passes, NSLOT=4, HALO_KV=True, GATE_IN_E=False, NXS=4):
    def _norm(lp):
        r = []
        for i_, p_ in enumerate(lp):
            r.append((p_[0], p_[1], p_[2] if len(p_) > 2 else ("first" if i_ == 0 else "cont")))
        return r
    layer_passes = [_norm(lp) for lp in layer_passes]
    NT = NBLK * 128
    NBMAX = max(p_[1] for lp in layer_passes for p_ in lp)
    TMAX = NBMAX * 128
    po = param_offsets(NL)
    NP = po["_n"]

    nc = bass.Bass("TRN2", target_bir_lowering=False)
    x_d = nc.dram_tensor("x", [NT, D], F32, kind="ExternalInput").ap()
    win_d = nc.dram_tensor("w_in", [NL, D, NCOL], F32, kind="ExternalInput").ap()
    wco_d = nc.dram_tensor("w_co", [NL, D, D], F32, kind="ExternalInput").ap()
    wao_d = nc.dram_tensor("w_ao", [NL, D, D], F32, kind="ExternalInput").ap()
    wo_d = nc.dram_tensor("w_o", [NL, D, D], F32, kind="ExternalInput").ap()
    par_d = nc.dram_tensor("params", [128, NP], F32, kind="ExternalInput").ap()
    out_d = nc.dram_tensor("out", [(NBLK - OUT0) * 128, D], F32, kind="ExternalOutput").ap()

    KVW = 4 * 128 + 4 * 65
    n_send = sum(1 for lp in layer_passes for p_ in lp if "send" in p_[2])
    xch = []
    for i_ in range(n_send):
        xch.append((
            nc.dram_tensor(f"xb_kv{i_}", [128, KVW], BF16, kind="Internal").ap(),
            nc.dram_tensor(f"xg_kv{i_}", [256, KVW], BF16, addr_space="Local", kind="Internal").ap(),
            nc.dram_tensor(f"xb_c{i_}", [128, 16], F32, kind="Internal").ap(),
            nc.dram_tensor(f"xg_c{i_}", [256, 16], F32, addr_space="Local", kind="Internal").ap(),
        ))
    S = Sched()
    es = ExitStack()
    E = es.enter_context
    xT = E(nc.sbuf_tensor("xT", [128, 8, NT], F32))
    hT = E(nc.sbuf_tensor("hT", [128, 8, TMAX], BF16))
    maT = E(nc.sbuf_tensor("maT", [128, 8, TMAX], BF16))
    uT = E(nc.sbuf_tensor("uT", [128, 8, TMAX], BF16))
    kT = E(nc.sbuf_tensor("kT", [128, 4, TMAX + 128], BF16))
    va = E(nc.sbuf_tensor("va", [128, NBMAX + 1, 4, 65], BF16))
    cv = E(nc.sbuf_tensor("cv", [128, TMAX + 2], F32))
    cvc = E(nc.sbuf_tensor("cvc", [128, 8, 2], F32))
    wsl = [E(nc.sbuf_tensor(f"wsl{i}", [128, 8, 256], BF16)) for i in range(NSLOT)]
    NT32 = 5
    t32 = E(nc.sbuf_tensor("t32", [128, NT32, 512], F32))
    NTB = 6
    tb = E(nc.sbuf_tensor("tb", [128, NTB, 512], BF16))
    on = [E(nc.sbuf_tensor(f"on{i}", [128, 1024], BF16)) for i in range(2)]
    den = [E(nc.sbuf_tensor(f"den{i}", [128, 4, 1], F32)) for i in range(2)]
    mpc = E(nc.sbuf_tensor("mpc", [128, 4, 128], BF16))
    mpcfr = E(nc.sbuf_tensor("mpcfr", [128, 4, 128], BF16))
    identf = E(nc.sbuf_tensor("identf", [128, 128], F32))
    identb = E(nc.sbuf_tensor("identb", [128, 128], BF16))
    onesb = E(nc.sbuf_tensor("onesb", [128, 128], BF16))
    bdb = E(nc.sbuf_tensor("bdb", [128, 128], BF16))
    xst = E(nc.sbuf_tensor("xst", [128, NXS, 1024], F32))
    par = E(nc.sbuf_tensor("par", [128, NP], F32))
    rcv = E(nc.sbuf_tensor("rcv", [128, 8, 2], F32))
    if GATE_IN_E:
        GT = max((p_[1] - (1 if (p_[2].startswith("first") and HALO_KV) else 0)) for lp in layer_passes for p_ in lp) * 128
        gbT = E(nc.sbuf_tensor("gbT", [128, 8, GT], BF16))
        hgb = E(nc.sbuf_tensor("hgb", [128, NL, 8], F32))
    NPS = 7
    psb = [E(nc.psum_tensor(f"ps{i}", [128, 512], F32)) for i in range(NPS)]
    pst = E(nc.psum_tensor("pst", [128, 8, 128], BF16))

    cnt = {"xs": 0, "ps": 0, "ps7": 0, "w": 0, "t32": 0, "tb": 0, "sE": 0, "oE": 0}

    def next_ps(full=False):
        if full:
            i = cnt["ps7"] % NPS
            cnt["ps7"] += 1
            return i
        i = cnt["ps"] % (NPS - 1)
        cnt["ps"] += 1
        return i

    def T32(i):
        return t32[:, i, :]

    def TB(i):
        return tb[:, i, :]

    S.add("sp", lambda e: e.dma_start(out=par[:], in_=par_d), writes=[("par",)], dma="par")
    S.add("pool", lambda e: e.memset(identf[:], 1.0), writes=[("identf",)])
    S.add("pool", lambda e: e.affine_select(out=identf[:], in_=identf[:], pattern=[[-1, 128]],
                                            compare_op=ALU.is_equal, fill=0.0, base=0, channel_multiplier=1),
          reads=[("identf",)], writes=[("identf",)])
    S.add("pool", lambda e: e.tensor_copy(out=identb[:], in_=identf[:]), reads=[("identf",)], writes=[("identb",)])
    S.add("pool", lambda e: e.memset(onesb[:], 1.0), writes=[("onesb",)])
    S.add("pool", lambda e: e.memset(bdb[:], 0.0), writes=[("bdb",)])
    S.add("pool", lambda e: e.memset(bdb[0:64, 0:64], 1.0), reads=[("bdb",)], writes=[("bdb",)])
    S.add("pool", lambda e: e.memset(bdb[64:128, 64:128], 1.0), reads=[("bdb",)], writes=[("bdb",)])
    S.add("pool", lambda e: e.memset(mpc[:], 1.0), writes=[("mpc",)])
    S.add("pool", lambda e: e.affine_select(out=mpc[:, 0:2, :], in_=mpc[:, 0:2, :], pattern=[[0, 2], [-1, 128]],
                                            compare_op=ALU.is_gt, fill=0.0, base=0, channel_multiplier=1),
          reads=[("mpc",)], writes=[("mpc",)])
    S.add("pool", lambda e: e.affine_select(out=mpc[:, 2:4, :], in_=mpc[:, 2:4, :], pattern=[[0, 2], [1, 128]],
                                            compare_op=ALU.is_ge, fill=0.0, base=0, channel_multiplier=-1),
          reads=[("mpc",)], writes=[("mpc",)])
    S.add("pool", lambda e: e.tensor_copy(out=mpcfr[:, 2:4, :], in_=mpc[:, 2:4, :]),
          reads=[("mpc",)], writes=[("mpcfr",)])
    S.add("pool", lambda e: e.tensor_scalar(out=mpcfr[:, 0:2, :], in0=mpc[:, 0:2, :], scalar1=par[:, po["fr"]:po["fr"] + 1],
                                            scalar2=None, op0=ALU.mult),
          reads=[("mpc",), ("par",), ("mpcfr",)], writes=[("mpcfr",)])
    S.add("dve", lambda e: e.tensor_scalar(out=par[:, po["qg"]:po["qg"] + NL], in0=par[:, po["qg"]:po["qg"] + NL],
                                           scalar1=0.125, scalar2=None, op0=ALU.mult),
          reads=[("par",)], writes=[("par2",)])
    S.add("act", lambda e: e.activation(out=par[:, po["sk"]:po["sk"] + NL * 16], in_=par[:, po["sk"]:po["sk"] + NL * 16],
                                        func=AF.Exp),
          reads=[("par",)], writes=[("par3",)])
    PARR = [("par",), ("par2",), ("par3",)]
    if GATE_IN_E:
        S.add("dve", lambda e: e.tensor_scalar(
            out=hgb[:], in0=par[:, po["gb"]:po["gb"] + NL * 16].rearrange("p (l j) -> p l j", j=16)[:, :, 8:16],
            scalar1=0.5, scalar2=None, op0=ALU.mult), reads=[("par",)], writes=[("hgb",)])
    S.add("pool", lambda e: e.memset(va[:], 1.0), writes=[("va", i) for i in range(NBMAX + 1)])
    S.add("pool", lambda e: e.memset(cvc[:], 0.0), writes=[("cvc", j) for j in range(8)])

    def stage(i):
        return xst[:, i, :]

    def load_x_blocks(blks, only_stage=None):
        for blk in blks:
            si = cnt["xs"] % NXS
            cnt["xs"] += 1
            S.add("sp", lambda e, blk=blk, si=si: e.dma_start(out=stage(si), in_=x_d[blk * 128:(blk + 1) * 128, :]),
                  writes=[("xst", si, 0), ("xst", si, 1)], dma=("st", si))
            for hh in range(2):
                pi = next_ps()
                for kk in range(4):
                    k = hh * 4 + kk
                    S.add("pe", lambda e, pi=pi, kk=kk, k=k, si=si: e.transpose(
                        psb[pi][:, kk * 128:(kk + 1) * 128], stage(si)[:, k * 128:(k + 1) * 128], identf[:]),
                        reads=[("xst", si, 0), ("xst", si, 1), ("identf",)], writes=[("ps", pi)])
                if hh == 0:
                    fn = lambda e, pi=pi, hh=hh, blk=blk: e.activation(
                        out=xT[:, hh * 4:hh * 4 + 4, blk * 128:(blk + 1) * 128],
                        in_=psb[pi][:].rearrange("p (a b) -> p a b", a=4), func=AF.Copy)
                    eng = "act"
                else:
                    fn = lambda e, pi=pi, hh=hh, blk=blk: e.tensor_copy(
                        out=xT[:, hh * 4:hh * 4 + 4, blk * 128:(blk + 1) * 128],
                        in_=psb[pi][:].rearrange("p (a b) -> p a b", a=4))
                    eng = "dve"
                S.add(eng, fn, reads=[("ps", pi)], writes=[("xT", k, blk) for k in range(hh * 4, hh * 4 + 4)])

    def store_blocks(blks):
        for blk in blks:
            si = cnt["xs"] % NXS
            cnt["xs"] += 1
            for hh in range(2):
                pi = next_ps()
                for kk in range(4):
                    k = hh * 4 + kk
                    S.add("pe", lambda e, pi=pi, kk=kk, k=k, blk=blk: e.transpose(
                        psb[pi][:, kk * 128:(kk + 1) * 128], xT[:, k, blk * 128:(blk + 1) * 128], identf[:]),
                        reads=[("xT", k, blk), ("identf",)], writes=[("ps", pi)])
                if hh == 0:
                    S.add("act", lambda e, pi=pi, si=si: e.activation(out=xst[:, si, 0:512], in_=psb[pi][:], func=AF.Copy),
                          reads=[("ps", pi)], writes=[("xst", si, 0)])
                else:
                    S.add("dve", lambda e, pi=pi, si=si: e.tensor_copy(out=xst[:, si, 512:1024], in_=psb[pi][:]),
                          reads=[("ps", pi)], writes=[("xst", si, 1)])
            ob = blk - OUT0
            S.add("sp", lambda e, si=si, ob=ob: e.dma_start(out=out_d[ob * 128:(ob + 1) * 128, :], in_=stage(si)),
                  reads=[("xst", si, 0), ("xst", si, 1)], writes=[("out", ob)], dma=("o", si))
            outs.append(("out", ob))

    outs = []
    first_b0, first_nb = layer_passes[0][0][0], layer_passes[0][0][1]
    early = list(range(first_b0, first_b0 + first_nb))
    late = [b_ for b_ in range(NBLK) if b_ not in early]
    start_gen = None

    def load_w(pieces):
        si = cnt["w"] % NSLOT
        cnt["w"] += 1

        def fn(e, si=si, pieces=pieces):
            r = []
            for (c0, n, src) in pieces:
                r.append(e.dma_start(out=wsl[si][:, :, c0:c0 + n], in_=src.rearrange("(k p) c -> p k c", p=128)))
            return r
        gate = []
        if 2 <= cnt["w"] - 1 < NSLOT:
            gate = [("xT", 7, early[-1])]
        S.add("pool", fn, reads=gate, writes=[("w", si)], dma=("w", si), dma_n=len(pieces))
        return si

    def tiles_of(nb):
        nt = (nb + 3) // 4
        base, extra = nb // nt, nb % nt
        r = []
        b = 0
        for i in range(nt):
            n = base + (1 if i < extra else 0)
            r.append((b, n))
            b += n
        return r

    def proj(si, ci, src, srckey, tb0, tn, full=False, bank=None):
        pi = bank if bank is not None else next_ps(full)
        n = tn * 128
        for k in range(8):
            S.add("pe", lambda e, pi=pi, si=si, ci=ci, k=k, tb0=tb0, n=n, src=src: e.matmul(
                psb[pi][:, 0:n], lhsT=wsl[si][:, k, ci * 128:(ci + 1) * 128],
                rhs=src[:, k, tb0 * 128:tb0 * 128 + n], start=(k == 0), stop=(k == 7)),
                reads=[("w", si)] + [(srckey, k, tb0 + j) for j in range(tn)], writes=[("ps", pi)])
        return pi

    def pass_tiles(NB, halo):
        if halo:
            return [(0, 1, True)] + [(1 + b_, n_, False) for (b_, n_) in tiles_of(NB - 1)]
        return [(b_, n_, False) for (b_, n_) in tiles_of(NB)]

    def phase0_steps(l, B0, NB, halo, which="all"):
        pi = NPS - 1
        for (tb0, tn, _hl) in pass_tiles(NB, halo):
            if (which == "halo" and not _hl) or (which == "nonhalo" and _hl):
                continue
            n = tn * 128
            g0 = (B0 + tb0) * 128
            for k in range(8):
                bi = cnt["tb"] % 2
                cnt["tb"] += 1
                S.add("act", lambda e, bi=bi, k=k, g0=g0, n=n: e.activation(
                    out=TB(bi)[:, 0:n], in_=xT[:, k, g0:g0 + n], func=AF.Square),
                    reads=[("xT", k, B0 + tb0 + j) for j in range(tn)], writes=[("tb", bi)])
                S.add("pe", lambda e, pi=pi, bi=bi, k=k, n=n: e.matmul(
                    psb[pi][:, 0:n], lhsT=onesb[:], rhs=TB(bi)[:, 0:n], start=(k == 0), stop=(k == 7)),
                    reads=[("tb", bi), ("onesb",)], writes=[("ps", pi)])
                yield
            ri = 4
            S.add("act", lambda e, pi=pi, n=n, ri=ri: e.activation(
                out=T32(ri)[:, 0:n], in_=psb[pi][:, 0:n], func=AF.Ln, bias=EPS, scale=1.0 / D),
                reads=[("ps", pi)], writes=[("t32", ri)])
            S.add("act", lambda e, n=n, ri=ri: e.activation(
                out=T32(ri)[:, 0:n], in_=T32(ri)[:, 0:n], func=AF.Exp, scale=-0.5),
                reads=[("t32", ri)], writes=[("t32", ri)])
            for k in range(8):
                S.add("dve", lambda e, k=k, g0=g0, n=n, tb0=tb0, ri=ri, l=l: e.scalar_tensor_tensor(
                    out=hT[:, k, tb0 * 128:tb0 * 128 + n], in0=xT[:, k, g0:g0 + n],
                    scalar=par[:, po["ng"] + l * 8 + k:po["ng"] + l * 8 + k + 1], in1=T32(ri)[:, 0:n],
                    op0=ALU.mult, op1=ALU.mult),
                    reads=[("xT", k, B0 + tb0 + j) for j in range(tn)] + [("t32", ri)] + PARR,
                    writes=[("hT", k, tb0 + j) for j in range(tn)])
            yield

    def advance(gen, nsteps):
        if gen is None:
            return None
        for _ in range(nsteps):
            try:
                next(gen)
            except StopIteration:
                return None
        return gen

    hoisted_next_layer = {}
    send_state = {"i": 0, "last": None}
    for l in range(NL):
        passes = layer_passes[l]
        for pidx, (B0, NB, mode) in enumerate(passes):
            T = NB * 128
            first_pass = mode.startswith("first")
            halo = 1 if (first_pass and HALO_KV) else 0
            tiles = pass_tiles(NB, halo)
            if mode == "remote":
                xb_kv, xg_kv, xb_c, xg_c = send_state["last"]
                S.add("sp", lambda e, xg_kv=xg_kv: e.dma_start(
                    out=kT[:, :, 0:128], in_=xg_kv[0:128, 0:512].rearrange("p (g t) -> p g t", g=4)),
                    reads=[("xg",)], writes=[("kT", c, h_, 0) for c in range(4) for h_ in range(2)], dma=("rx", 0))
                S.add("sp", lambda e, xg_kv=xg_kv: e.dma_start(
                    out=va[:, 0, :, :], in_=xg_kv[0:128, 512:KVW].rearrange("p (g d) -> p g d", g=4)),
                    reads=[("xg",)], writes=[("va", 0)], dma=("rx", 1))
                S.add("sp", lambda e, xg_c=xg_c: e.dma_start(
                    out=rcv[:].rearrange("p a b -> p (a b)"), in_=xg_c[0:128, :]),
                    reads=[("xgc",)], writes=[("rcv",)], dma=("rx", 2))
                S.add("dve", lambda e: e.tensor_scalar(out=cvc[:], in0=rcv[:], scalar1=par[:, po["fr"]:po["fr"] + 1],
                                                       scalar2=None, op0=ALU.mult),
                      reads=[("rcv",), ("par",)], writes=[("cvc", j) for j in range(8)])
            if first_pass:
                S.add("pool", lambda e: e.memset(kT[:, :, 0:128], 0.0), writes=[("kT", c, h_, 0) for c in range(4) for h_ in range(2)])
                S.add("pool", lambda e: e.memset(va[:, 0, :, 0:64], 0.0), writes=[("va", 0)])
                S.add("pool", lambda e: e.memset(cvc[:], 0.0), writes=[("cvc", j) for j in range(8)])

            if l == 0 and pidx == 0:
                g_ = phase0_steps(l, B0, NB, halo)
                for (tb0_, tn_, _h) in tiles:
                    load_x_blocks(list(range(B0 + tb0_, B0 + tb0_ + tn_)))
                    g_ = advance(g_, 9)
                advance(g_, 10 ** 6)
            elif pidx == 0 and hoisted_next_layer.get(l, None) == "nonhalo":
                advance(phase0_steps(l, B0, NB, halo, "halo"), 10 ** 6)
            elif pidx == 0 and not hoisted_next_layer.get(l, False):
                advance(phase0_steps(l, B0, NB, halo), 10 ** 6)
            late_q = list(late) if (l == 0 and pidx == 0) else []

            for j in range(8):
                sA = load_w([(0, 128, win_d[l, :, C_V + j * 128:C_V + (j + 1) * 128]),
                             (128, 128, win_d[l, :, C_C + j * 128:C_C + (j + 1) * 128])])
                sB = load_w([(0, 128, win_d[l, :, C_B + j * 128:C_B + (j + 1) * 128]),
                             (128, 128, win_d[l, :, C_Z + j * 128:C_Z + (j + 1) * 128])])
                S.add("dve", lambda e, j=j: e.tensor_copy(out=cv[:, 0:2], in_=cvc[:, j, :]),
                      reads=[("cvc", j)], writes=[("cvh",)])
                for (tb0, tn, hl) in tiles:
                    n = tn * 128
                    t0 = tb0 * 128
                    pv = proj(sA, 0, hT, "hT", tb0, tn)
                    pc = proj(sA, 1, hT, "hT", tb0, tn)
                    vi = cnt["t32"] % 2
                    cnt["t32"] += 1
                    zi = 2 + vi
                    yi = 4
                    S.add("act", lambda e, pv=pv, vi=vi, n=n: e.activation(out=T32(vi)[:, 0:n], in_=psb[pv][:, 0:n], func=AF.Copy),
                          reads=[("ps", pv)], writes=[("t32", vi)])
                    S.add("dve", lambda e, pc=pc, vi=vi, n=n, t0=t0: e.tensor_tensor(
                        out=cv[:, 2 + t0:2 + t0 + n], in0=psb[pc][:, 0:n], in1=T32(vi)[:, 0:n], op=ALU.mult),
                        reads=[("ps", pc), ("t32", vi)], writes=[("cvb", tb0)])
                    if hl:
                        continue
                    pb = proj(sB, 0, hT, "hT", tb0, tn)
                    pz = proj(sB, 1, hT, "hT", tb0, tn)
                    S.add("act", lambda e, pz=pz, zi=zi, n=n: e.activation(out=T32(zi)[:, 0:n], in_=psb[pz][:, 0:n], func=AF.Silu),
                          reads=[("ps", pz)], writes=[("t32", zi)])
                    rd_cv = [("cvb", tb0), ("cvh",)] + ([("cvb", tiles[[t[0] for t in tiles].index(tb0) - 1][0])] if tb0 > 0 else [])
                    cwo = po["cw"] + l * 24
                    S.add("dve", lambda e, n=n, t0=t0, yi=yi, j=j, cwo=cwo: e.tensor_scalar(
                        out=T32(yi)[:, 0:n], in0=cv[:, t0:t0 + n], scalar1=par[:, cwo + j:cwo + j + 1],
                        scalar2=None, op0=ALU.mult),
                        reads=rd_cv + PARR, writes=[("t32", yi)])
                    for kk in (1, 2):
                        S.add("dve", lambda e, n=n, t0=t0, yi=yi, j=j, cwo=cwo, kk=kk: e.scalar_tensor_tensor(
                            out=T32(yi)[:, 0:n], in0=cv[:, t0 + kk:t0 + kk + n],
                            scalar=par[:, cwo + kk * 8 + j:cwo + kk * 8 + j + 1], in1=T32(yi)[:, 0:n],
                            op0=ALU.mult, op1=ALU.add),
                            reads=rd_cv + [("t32", yi)] + PARR, writes=[("t32", yi)])
                    S.add("dve", lambda e, n=n, yi=yi, pb=pb: e.tensor_tensor(
                        out=T32(yi)[:, 0:n], in0=psb[pb][:, 0:n], in1=T32(yi)[:, 0:n], op=ALU.mult),
                        reads=[("ps", pb), ("t32", yi)], writes=[("t32", yi)])
                    S.add("dve", lambda e, n=n, yi=yi, zi=zi, j=j, t0=t0: e.tensor_tensor(
                        out=uT[:, j, t0:t0 + n], in0=T32(yi)[:, 0:n], in1=T32(zi)[:, 0:n], op=ALU.mult),
                        reads=[("t32", yi), ("t32", zi)], writes=[("uT", j, tb0 + jj) for jj in range(tn)])
                lt = tiles[-1][0]
                S.add("dve", lambda e, j=j, T=T: e.tensor_copy(out=cvc[:, j, :], in_=cv[:, T:T + 2]),
                      reads=[("cvb", lt)], writes=[("cvc", j)])
                if late_q:
                    load_x_blocks([late_q.pop(0)])
            if late_q:
                load_x_blocks(late_q)
                late_q = []

            for ep in range(4):
                s1 = load_w([(0, 256, wco_d[l, :, ep * 256:(ep + 1) * 256])])
                s2 = load_w([(0, 256, win_d[l, :, C_G + ep * 256:C_G + (ep + 1) * 256])])
                for ee in range(2):
                    e_ = ep * 2 + ee
                    for (tb0, tn, hl) in tiles:
                        if hl:
                            continue
                        n = tn * 128
                        t0 = tb0 * 128
                        pga = proj(s2, ee, hT, "hT", tb0, tn)
                        pya = proj(s1, ee, uT, "uT", tb0, tn)
                        gi = cnt["t32"] % 2
                        cnt["t32"] += 1
                        gbo = po["gb"] + l * 16 + e_
                        S.add("act", lambda e, pga=pga, gi=gi, n=n, gbo=gbo: e.activation(
                            out=T32(gi)[:, 0:n], in_=psb[pga][:, 0:n], func=AF.Sigmoid, bias=par[:, gbo:gbo + 1], scale=1.0),
                            reads=[("ps", pga)] + PARR, writes=[("t32", gi)])
                        S.add("dve", lambda e, pya=pya, gi=gi, n=n, e_=e_, t0=t0: e.tensor_tensor(
                            out=maT[:, e_, t0:t0 + n], in0=psb[pya][:, 0:n], in1=T32(gi)[:, 0:n], op=ALU.mult),
                            reads=[("ps", pya), ("t32", gi)], writes=[("maT", e_, tb0 + jj) for jj in range(tn)])

            units = []
            for cp in range(4):
                units.append(("q", cp))
            units.append(("k", 0))
            pend = None

            def finish_qk(pd):
                pq, bi, kind, c, tb0, tn = pd
                n = tn * 128
                t0 = tb0 * 128
                pss = next_ps(True)
                S.add("pe", lambda e, pss=pss, bi=bi, n=n: e.matmul(
                    psb[pss][:, 0:n], lhsT=bdb[:], rhs=TB(bi)[:, 0:n], start=True, stop=True),
                    reads=[("tb", bi), ("bdb",)], writes=[("ps", pss)])
                ri = cnt["t32"] % 2
                cnt["t32"] += 1
                S.add("act", lambda e, pss=pss, ri=ri, n=n: e.activation(
                    out=T32(ri)[:, 0:n], in_=psb[pss][:, 0:n], func=AF.Ln, bias=EPS, scale=1.0 / 64),
                    reads=[("ps", pss)], writes=[("t32", ri)])
                S.add("act", lambda e, ri=ri, n=n: e.activation(
                    out=T32(ri)[:, 0:n], in_=T32(ri)[:, 0:n], func=AF.Exp, scale=-0.5),
                    reads=[("t32", ri)], writes=[("t32", ri)])
                if kind == "q":
                    go = po["qg"] + l
                    S.add("dve", lambda e, pq=pq, ri=ri, n=n, c=c, t0=t0, go=go: e.scalar_tensor_tensor(
                        out=uT[:, c, t0:t0 + n], in0=psb[pq][:, 0:n], scalar=par[:, go:go + 1], in1=T32(ri)[:, 0:n],
                        op0=ALU.mult, op1=ALU.mult),
                        reads=[("ps", pq), ("t32", ri)] + PARR, writes=[("uT", c, tb0 + jj) for jj in range(tn)])
                else:
                    go = po["kg"] + l
                    for h_ in range(2):
                        p0, p1 = h_ * 64, (h_ + 1) * 64
                        S.add("dve", lambda e, pq=pq, ri=ri, n=n, c=c, t0=t0, go=go, p0=p0, p1=p1, h_=h_: e.scalar_tensor_tensor(
                            out=kT[p0:p1, 2 * c + h_, 128 + t0:128 + t0 + n], in0=psb[pq][p0:p1, 0:n],
                            scalar=par[p0:p1, go:go + 1], in1=T32(ri)[p0:p1, 0:n], op0=ALU.mult, op1=ALU.mult),
                            reads=[("ps", pq), ("t32", ri)] + PARR,
                            writes=[("kT", 2 * c + h_, h_, 1 + tb0 + jj) for jj in range(tn)])

            for (kind, idx) in units:
                if kind == "q":
                    si = load_w([(0, 256, win_d[l, :, C_Q + idx * 256:C_Q + (idx + 1) * 256])])
                else:
                    si = load_w([(0, 256, win_d[l, :, C_K:C_K + 256])])
                for cc in range(2):
                    c = idx * 2 + cc
                    for (tb0, tn, hl) in tiles:
                        if hl and kind == "q":
                            continue
                        n = tn * 128
                        pq = proj(si, cc, hT, "hT", tb0, tn, True)
                        bi = 2 + cnt["tb"] % 2
                        cnt["tb"] += 1
                        S.add("act", lambda e, pq=pq, bi=bi, n=n: e.activation(
                            out=TB(bi)[:, 0:n], in_=psb[pq][:, 0:n], func=AF.Square),
                            reads=[("ps", pq)], writes=[("tb", bi)])
                        if pend is not None:
                            finish_qk(pend)
                        pend = (pq, bi, kind, c, tb0, tn)
            finish_qk(pend)
            pend = None
            for g in range(4):
                own = g % 2
                oth = 1 - own
                S.add("sp", lambda e, g=g, own=own, oth=oth, T=T: e.dma_start(
                    out=kT[oth * 64:(oth + 1) * 64, g, 128:128 + T], in_=kT[own * 64:(own + 1) * 64, g, 128:128 + T]),
                    reads=[("kT", g, own, 1 + jj) for jj in range(NB)], writes=[("kT", g, oth, 1 + jj) for jj in range(NB)],
                    dma=("kd", g))

            sv = load_w([(0, 256, win_d[l, :, C_VV:C_VV + 256])])

            def vproj(b, pi):
                for k in range(8):
                    S.add("pe", lambda e, pi=pi, k=k, b=b, sv=sv: e.matmul(
                        psb[pi][:, 0:256], lhsT=hT[:, k, b * 128:(b + 1) * 128], rhs=wsl[sv][:, k, 0:256],
                        start=(k == 0), stop=(k == 7)),
                        reads=[("w", sv), ("hT", k, b)], writes=[("ps", pi)])
                S.add("act", lambda e, pi=pi, b=b: e.activation(
                    out=va[:, b + 1, :, 0:64], in_=psb[pi][:, 0:256].rearrange("p (g d) -> p g d", g=4), func=AF.Copy),
                    reads=[("ps", pi)], writes=[("va", b + 1)])

            DLATE = 3 if NB - halo >= 5 else 0
            for b in range(NB - DLATE):
                vproj(b, next_ps())

            def send_halo():
                xb_kv, xg_kv, xb_c, xg_c = xch[send_state["i"]]
                send_state["i"] += 1
                send_state["last"] = (xb_kv, xg_kv, xb_c, xg_c)
                sidx = send_state["i"]
                S.add("sp", lambda e, xb_kv=xb_kv, NB=NB: e.dma_start(
                    out=xb_kv[:, 0:512].rearrange("p (g t) -> p g t", g=4), in_=kT[:, :, NB * 128:(NB + 1) * 128]),
                    reads=[("kT", c, h_, NB) for c in range(4) for h_ in range(2)], writes=[("xb",)], dma=("tx", 0))
                S.add("sp", lambda e, xb_kv=xb_kv, NB=NB: e.dma_start(
                    out=xb_kv[:, 512:KVW].rearrange("p (g d) -> p g d", g=4), in_=va[:, NB, :, :]),
                    reads=[("va", NB)], writes=[("xb2",)], dma=("tx", 1))
                S.add("sp", lambda e, xb_c=xb_c: e.dma_start(out=xb_c, in_=cvc[:].rearrange("p a b -> p (a b)")),
                      reads=[("cvc", j) for j in range(8)], writes=[("xbc",)], dma=("tx", 2))
                RG = [[0, 1], [2, 3], [4, 5], [6, 7]]
                S.add("pool", lambda e, xb_kv=xb_kv, xg_kv=xg_kv: e.collective_compute(
                    "AllGather", ALU.bypass, replica_groups=RG, ins=[xb_kv], outs=[xg_kv]),
                    reads=[("xb",), ("xb2",)], writes=[("xg",)], dma=("cc", sidx, 0), dma_inc=1)
                S.add("pool", lambda e, xb_c=xb_c, xg_c=xg_c: e.collective_compute(
                    "AllGather", ALU.bypass, replica_groups=RG, ins=[xb_c], outs=[xg_c]),
                    reads=[("xbc",)], writes=[("xgc",)], dma=("cc", sidx, 1), dma_inc=1)

            def load_za(cp):
                return load_w([(0, 256, win_d[l, :, C_ZA + cp * 256:C_ZA + (cp + 1) * 256])])
            if GATE_IN_E:
                gb_slots = [load_w([(0, 256, win_d[l, :, C_G + D + ep * 256:C_G + D + (ep + 1) * 256])]) for ep in range(4)]
                za_slots = []
                ntl = sum(1 for t_ in tiles if not t_[2])
                gsteps = [(ep, ee, tb0, tn) for ep in range(4) for ee in range(2) for (tb0, tn, hl) in tiles if not hl]
                gdone = {"n": 0}

                gcur = {"st": None, "half": 0}

                def gate_half():
                    if gcur["st"] is None:
                        gcur["st"] = gsteps.pop(0)
                        gcur["half"] = 0
                    ep, ee, tb0, tn = gcur["st"]
                    n = tn * 128
                    e_ = ep * 2 + ee
                    g0_ = (tb0 - halo) * 128
                    pgb = NPS - 1
                    si = gb_slots[ep]
                    for k in range(4 * gcur["half"], 4 * gcur["half"] + 4):
                        S.add("pe", lambda e, pgb=pgb, si=si, ee=ee, k=k, tb0=tb0, n=n: e.matmul(
                            psb[pgb][:, 0:n], lhsT=wsl[si][:, k, ee * 128:(ee + 1) * 128],
                            rhs=hT[:, k, tb0 * 128:tb0 * 128 + n], start=(k == 0), stop=(k == 7)),
                            reads=[("w", si)] + [("hT", k, tb0 + j) for j in range(tn)], writes=[("ps", pgb)])
                    if gcur["half"] == 0:
                        gcur["half"] = 1
                        return
                    S.add("act", lambda e, pgb=pgb, n=n, e_=e_, g0_=g0_, l=l: e.activation(
                        out=gbT[:, e_, g0_:g0_ + n], in_=psb[pgb][:, 0:n], func=AF.Tanh,
                        bias=hgb[:, l, e_:e_ + 1], scale=0.5),
                        reads=[("ps", pgb), ("hgb",)], writes=[("gbT", e_, tb0 + jj) for jj in range(tn)])
                    gcur["st"] = None
                    gdone["n"] += 1
                    if gdone["n"] % (2 * ntl) == 0 and len(za_slots) < 4:
                        za_slots.append(load_za(len(za_slots)))

                def gate_pending():
                    return bool(gsteps) or gcur["st"] is not None
            else:
                za_slots = [load_za(cp) for cp in range(min(4, NSLOT))]
            def qk(b, g):
                bx, by = (cnt["sE"] % 2) * 2, (cnt["sE"] % 2) * 2 + 1
                cnt["sE"] += 1
                for (slot, coff) in ((b, 0), (b + 1, 256)):
                    for (hf, pi) in ((0, bx), (1, by)):
                        S.add("pe", lambda e, pi=pi, slot=slot, coff=coff, hf=hf, g=g, b=b: e.matmul(
                            psb[pi][:, coff:coff + 256].rearrange("p (a t) -> p a t", a=2),
                            lhsT=kT[hf * 64:(hf + 1) * 64, g, slot * 128:(slot + 1) * 128],
                            rhs=uT[hf * 64:(hf + 1) * 64, 2 * g:2 * g + 2, b * 128:(b + 1) * 128],
                            start=True, stop=True, tile_position=(hf * 64, 0)),
                            reads=[("kT", g, hf, slot), ("uT", 2 * g, b), ("uT", 2 * g + 1, b)], writes=[("ps", pi)])
                return (bx, by)

            def expmask(b, g, bx, by, u):
                gblk = B0 + b
                pt = (2 * (u % 3), 2 * (u % 3) + 1)
                mk, mkey = (mpcfr, ("mpcfr",)) if gblk == FR else (mpc, ("mpc",))
                for (pi, ti, meng) in ((bx, pt[0], "pool"), (by, pt[1], "dve")):
                    S.add("act", lambda e, pi=pi, ti=ti: e.activation(out=TB(ti), in_=psb[pi][:], func=AF.Exp),
                          reads=[("ps", pi)], writes=[("tb", ti)])
                    if meng == "pool" and gblk != FR:
                        tv = TB(ti).rearrange("p (c a t) -> p c a t", c=2, a=2)
                        S.add("pool", lambda e, tv=tv: e.affine_select(
                            out=tv[:, 0, :, :], in_=tv[:, 0, :, :], pattern=[[0, 2], [-1, 128]],
                            compare_op=ALU.is_gt, fill=0.0, base=0, channel_multiplier=1),
                            reads=[("tb", ti)], writes=[("tb", ti)])
                        S.add("pool", lambda e, tv=tv: e.affine_select(
                            out=tv[:, 1, :, :], in_=tv[:, 1, :, :], pattern=[[0, 2], [1, 128]],
                            compare_op=ALU.is_ge, fill=0.0, base=0, channel_multiplier=-1),
                            reads=[("tb", ti)], writes=[("tb", ti)])
                        continue
                    S.add(meng, lambda e, ti=ti, mk=mk: e.tensor_tensor(
                        out=TB(ti), in0=TB(ti), in1=mk[:].rearrange("p a t -> p (a t)"), op=ALU.mult),
                        reads=[("tb", ti), mkey], writes=[("tb", ti)])

            def softmax_pv(b, g, bx, by, u):
                pt = (2 * (u % 3), 2 * (u % 3) + 1)
                po_ = 4 + cnt["oE"] % (2 if GATE_IN_E else 3)
                cnt["oE"] += 1
                ov = psb[po_][:, 0:260].rearrange("p (i d) -> p i d", d=65)
                for a_ in range(2):
                    for hf in range(2):
                        i = 2 * a_ + hf
                        ti = pt[hf]
                        for (coff, slot, st, sp2) in ((0, b, True, False), (256, b + 1, False, True)):
                            S.add("pe", lambda e, ti=ti, slot=slot, st=st, sp2=sp2, i=i, g=g, ov=ov, coff=coff, a_=a_: e.matmul(
                                ov[:, i, :], lhsT=TB(ti)[:, coff + a_ * 128:coff + (a_ + 1) * 128], rhs=va[:, slot, g, :],
                                start=st, stop=sp2),
                                reads=[("tb", ti), ("va", slot)], writes=[("ps", po_)])
                di = u % 2
                oi = b % 2
                sko = po["sk"] + l * 16 + 4 * g
                S.add("dve", lambda e, di=di, ov=ov, sko=sko: e.tensor_tensor(
                    out=den[di][:], in0=ov[:, :, 64:65], in1=par[:, sko:sko + 4].rearrange("p (a b) -> p a b", b=1), op=ALU.add),
                    reads=[("ps", po_)] + PARR, writes=[("den", di)])
                S.add("dve", lambda e, di=di: e.reciprocal(out=den[di][:], in_=den[di][:]),
                      reads=[("den", di)], writes=[("den", di)])
                S.add("dve", lambda e, di=di, ov=ov, oi=oi, g=g: e.tensor_tensor(
                    out=on[oi][:, g * 256:(g + 1) * 256].rearrange("p (i d) -> p i d", d=64), in0=ov[:, :, 0:64],
                    in1=den[di][:].broadcast_to([128, 4, 64]), op=ALU.mult),
                    reads=[("ps", po_), ("den", di)], writes=[("on", oi, g)])

            def finish_block(b):
                oi = b % 2
                for c in range(8):
                    S.add("pe", lambda e, c=c, oi=oi: e.transpose(pst[:, c, :], on[oi][:, c * 128:(c + 1) * 128], identb[:]),
                          reads=[("on", oi, c // 2), ("identb",)], writes=[("pst",)])
                S.add("act", lambda e, b=b: e.activation(out=uT[:, :, b * 128:(b + 1) * 128], in_=pst[:], func=AF.Copy),
                      reads=[("pst",)], writes=[("uT", c, b) for c in range(8)])

            seq = [(b, g) for b in range(halo, NB) for g in range(4)]
            units = {}
            nun = len(seq)
            for it in range(nun + 3):
                if GATE_IN_E and it >= 1 and gate_pending():
                    gate_half()
                if it < nun:
                    b, g = seq[it]
                    units[it] = (b, g) + qk(b, g)
                if 0 <= it - 1 < nun:
                    b1, g1, x1, y1 = units[it - 1]
                    expmask(b1, g1, x1, y1, it - 1)
                if it == 2:
                    for i_, b in enumerate(range(NB - DLATE, NB)):
                        vproj(b, 4 + i_ % (2 if GATE_IN_E else 3))
                    if "send" in mode:
                        send_halo()
                    if GATE_IN_E:
                        za_slots.append(load_za(0))
                if 0 <= it - 3 < nun:
                    b2, g2, x2, y2 = units[it - 3]
                    softmax_pv(b2, g2, x2, y2, it - 3)
                    if g2 == 2 and b2 > halo:
                        finish_block(b2 - 1)
            finish_block(NB - 1)
            if GATE_IN_E:
                while gate_pending():
                    gate_half()
                while len(za_slots) < 4:
                    za_slots.append(load_za(len(za_slots)))
            nxt_mode = passes[pidx + 1][2] if pidx + 1 < len(passes) else None
            if nxt_mode == "cont":
                S.add("pool", lambda e, NB=NB: e.tensor_copy(out=kT[:, :, 0:128], in_=kT[:, :, NB * 128:(NB + 1) * 128]),
                      reads=[("kT", c, h_, NB) for c in range(4) for h_ in range(2)],
                      writes=[("kT", c, h_, 0) for c in range(4) for h_ in range(2)])
                S.add("pool", lambda e, NB=NB: e.tensor_copy(out=va[:, 0, :, :], in_=va[:, NB, :, :]),
                      reads=[("va", NB)], writes=[("va", 0)])

            for cp in range(4):
                if cp < len(za_slots):
                    si = za_slots[cp]
                else:
                    si = load_za(cp)
                for cc in range(2):
                    c = cp * 2 + cc
                    for (tb0, tn, hl) in tiles:
                        if hl:
                            continue
                        n = tn * 128
                        t0 = tb0 * 128
                        pz = proj(si, cc, hT, "hT", tb0, tn)
                        zi = cnt["t32"] % 2
                        cnt["t32"] += 1
                        S.add("act", lambda e, pz=pz, zi=zi, n=n: e.activation(out=T32(zi)[:, 0:n], in_=psb[pz][:, 0:n], func=AF.Silu),
                              reads=[("ps", pz)], writes=[("t32", zi)])
                        S.add("dve", lambda e, zi=zi, n=n, c=c, t0=t0: e.tensor_tensor(
                            out=uT[:, c, t0:t0 + n], in0=uT[:, c, t0:t0 + n], in1=T32(zi)[:, 0:n], op=ALU.mult),
                            reads=[("t32", zi)] + [("uT", c, tb0 + jj) for jj in range(tn)],
                            writes=[("uT", c, tb0 + jj) for jj in range(tn)])

            if GATE_IN_E:
                for ep in range(4):
                    s1 = load_w([(0, 256, wao_d[l, :, ep * 256:(ep + 1) * 256])])
                    for ee in range(2):
                        e_ = ep * 2 + ee
                        for (tb0, tn, hl) in tiles:
                            if hl:
                                continue
                            n = tn * 128
                            t0 = tb0 * 128
                            g0_ = (tb0 - halo) * 128
                            pyb = proj(s1, ee, uT, "uT", tb0, tn)
                            ti = 2 + cnt["t32"] % 2
                            cnt["t32"] += 1
                            S.add("dve", lambda e, pyb=pyb, ti=ti, n=n, e_=e_, g0_=g0_: e.scalar_tensor_tensor(
                                out=T32(ti)[:, 0:n], in0=gbT[:, e_, g0_:g0_ + n], scalar=1.0, in1=psb[pyb][:, 0:n],
                                op0=ALU.add, op1=ALU.mult),
                                reads=[("ps", pyb)] + [("gbT", e_, tb0 + jj) for jj in range(tn)], writes=[("t32", ti)])
                            S.add("dve", lambda e, ti=ti, n=n, e_=e_, t0=t0: e.scalar_tensor_tensor(
                                out=maT[:, e_, t0:t0 + n], in0=T32(ti)[:, 0:n], scalar=0.5, in1=maT[:, e_, t0:t0 + n],
                                op0=ALU.mult, op1=ALU.add),
                                reads=[("t32", ti)] + [("maT", e_, tb0 + jj) for jj in range(tn)],
                                writes=[("maT", e_, tb0 + jj) for jj in range(tn)])
            else:
                for ep in range(4):
                    s1 = load_w([(0, 256, wao_d[l, :, ep * 256:(ep + 1) * 256])])
                    s2 = load_w([(0, 256, win_d[l, :, C_G + D + ep * 256:C_G + D + (ep + 1) * 256])])
                    for ee in range(2):
                        e_ = ep * 2 + ee
                        for (tb0, tn, hl) in tiles:
                            if hl:
                                continue
                            n = tn * 128
                            t0 = tb0 * 128
                            pgb = proj(s2, ee, hT, "hT", tb0, tn)
                            pyb = proj(s1, ee, uT, "uT", tb0, tn)
                            gi = cnt["t32"] % 2
                            cnt["t32"] += 1
                            ti = 2 + gi
                            gbo = po["gb"] + l * 16 + 8 + e_
                            S.add("act", lambda e, pgb=pgb, gi=gi, n=n, gbo=gbo: e.activation(
                                out=T32(gi)[:, 0:n], in_=psb[pgb][:, 0:n], func=AF.Sigmoid, bias=par[:, gbo:gbo + 1], scale=1.0),
                                reads=[("ps", pgb)] + PARR, writes=[("t32", gi)])
                            S.add("dve", lambda e, pyb=pyb, gi=gi, ti=ti, n=n: e.tensor_tensor(
                                out=T32(ti)[:, 0:n], in0=psb[pyb][:, 0:n], in1=T32(gi)[:, 0:n], op=ALU.mult),
                                reads=[("ps", pyb), ("t32", gi)], writes=[("t32", ti)])
                            S.add("dve", lambda e, ti=ti, n=n, e_=e_, t0=t0: e.tensor_tensor(
                                out=maT[:, e_, t0:t0 + n], in0=maT[:, e_, t0:t0 + n], in1=T32(ti)[:, 0:n], op=ALU.add),
                                reads=[("t32", ti)] + [("maT", e_, tb0 + jj) for jj in range(tn)],
                                writes=[("maT", e_, tb0 + jj) for jj in range(tn)])

            gen0 = None
            if pidx + 1 < len(passes):
                gen0 = phase0_steps(l, passes[pidx + 1][0], passes[pidx + 1][1], 0)
            elif l + 1 < NL:
                nB0, nNB = layer_passes[l + 1][0][0], layer_passes[l + 1][0][1]
                if nB0 + nNB <= B0 or nB0 >= B0 + NB:
                    gen0 = phase0_steps(l + 1, nB0, nNB, 1 if HALO_KV else 0)
                    hoisted_next_layer[l + 1] = True
                elif HALO_KV and (nB0 + 1 >= B0 + NB or nB0 + nNB <= B0):
                    gen0 = phase0_steps(l + 1, nB0, nNB, 1, "nonhalo")
                    hoisted_next_layer[l + 1] = "nonhalo"
            last_pass_of_all = (l == NL - 1 and pidx + 1 == len(passes) and NSLOT >= 4)
            if last_pass_of_all:
                wos = [load_w([(0, 256, wo_d[l, :, fp * 256:(fp + 1) * 256])]) for fp in range(4)]
                for (tb0, tn, hl) in tiles:
                    if hl:
                        continue
                    n = tn * 128
                    g0 = (B0 + tb0) * 128
                    for f in range(8):
                        pp = proj(wos[f // 2], f % 2, maT, "maT", tb0, tn)
                        S.add("dve", lambda e, pp=pp, n=n, f=f, g0=g0: e.tensor_tensor(
                            out=xT[:, f, g0:g0 + n], in0=psb[pp][:, 0:n], in1=xT[:, f, g0:g0 + n], op=ALU.add),
                            reads=[("ps", pp)] + [("xT", f, B0 + tb0 + jj) for jj in range(tn)],
                            writes=[("xT", f, B0 + tb0 + jj) for jj in range(tn)])
                    store_blocks([b_ for b_ in range(B0 + tb0, B0 + tb0 + tn) if b_ >= OUT0])
            else:
                for fp in range(4):
                    si = load_w([(0, 256, wo_d[l, :, fp * 256:(fp + 1) * 256])])
                    for ff in range(2):
                        f = fp * 2 + ff
                        for (tb0, tn, hl) in tiles:
                            if hl:
                                continue
                            n = tn * 128
                            g0 = (B0 + tb0) * 128
                            pp = proj(si, ff, maT, "maT", tb0, tn)
                            S.add("dve", lambda e, pp=pp, n=n, f=f, g0=g0: e.tensor_tensor(
                                out=xT[:, f, g0:g0 + n], in0=psb[pp][:, 0:n], in1=xT[:, f, g0:g0 + n], op=ALU.add),
                                reads=[("ps", pp)] + [("xT", f, B0 + tb0 + jj) for jj in range(tn)],
                                writes=[("xT", f, B0 + tb0 + jj) for jj in range(tn)])
                            gen0 = advance(gen0, 2)
                advance(gen0, 10 ** 6)

            if l == NL - 1 and pidx + 1 < len(passes):
                store_blocks([b_ for b_ in range(max(OUT0, B0 + halo), B0 + NB)])

    store_blocks([b_ for b_ in range(OUT0, NBLK) if ("out", b_ - OUT0) not in outs])
    S.add("sp", None, reads=outs)

    S.emit(nc, es)
    es.close()
    return nc


_CACHE = {}
FUSED = True


def _get_nc(key, *args):
    if key not in _CACHE:
        _CACHE[key] = build(*args)
    return _CACHE[key]


def kernel(x, norm_g, w_in, conv_w, q_norm_g, k_norm_g, sinks, w_conv_out, w_attn_out, gate_b, w_out):
    x = np.ascontiguousarray(np.asarray(x, dtype=np.float32))
    f = lambda a: np.ascontiguousarray(np.asarray(a, dtype=np.float32))
    norm_g, w_in, conv_w, q_norm_g, k_norm_g, sinks = map(f, (norm_g, w_in, conv_w, q_norm_g, k_norm_g, sinks))
    w_conv_out, w_attn_out, gate_b, w_out = map(f, (w_conv_out, w_attn_out, gate_b, w_out))
    B, SEQ, _ = x.shape
    L = norm_g.shape[0]
    HALF = SEQ // 2
    n = 8
    if not FUSED:
        HALO = 128
        NBLK = (HALF + HALO) // 128
        nb1 = (NBLK + 1) // 2
        passes = [(0, nb1), (nb1, NBLK - nb1)]
        nc = _get_nc(("unf", NBLK), 1, NBLK, 1, 1, [passes], 4)
        cur = x
        for l in range(L):
            in_maps = []
            for c in range(n):
                b, hf = c // 2, c % 2
                xs = np.zeros((HALF + HALO, D), np.float32)
                if hf == 0:
                    xs[HALO:] = cur[b, 0:HALF]
                else:
                    xs[:] = cur[b, HALF - HALO:SEQ]
                in_maps.append({
                    "x": xs, "w_in": w_in[l:l + 1], "w_co": w_conv_out[l:l + 1], "w_ao": w_attn_out[l:l + 1],
                    "w_o": w_out[l:l + 1],
                    "params": pack_params(norm_g[l:l + 1], gate_b[l:l + 1], conv_w[l:l + 1], q_norm_g[l:l + 1],
                                          k_norm_g[l:l + 1], sinks[l:l + 1], float(hf)),
                })
            res = run_bass_kernel_spmd(nc, in_maps, core_ids=list(range(n)))
            nxt = np.empty_like(cur)
            for c in range(n):
                b, hf = c // 2, c % 2
                nxt[b, hf * HALF:(hf + 1) * HALF] = res.results[c]["out"]
            cur = nxt
        return cur
    else:
        NBLK = HALF // 128
        nbA = NBLK // 2
        lp = [[(nbA - 1, NBLK - nbA + 1, "first+send"), (0, nbA, "remote")] for l in range(L)]
        nc = _get_nc(("fused", NBLK, L), L, NBLK, 0, 0, lp, 5, True, True, 2)
        in_maps = []
        for c in range(n):
            b, hf = c // 2, c % 2
            in_maps.append({
                "x": np.ascontiguousarray(x[b, hf * HALF:(hf + 1) * HALF]),
                "w_in": w_in, "w_co": w_conv_out, "w_ao": w_attn_out, "w_o": w_out,
                "params": pack_params(norm_g, gate_b, conv_w, q_norm_g, k_norm_g, sinks, float(hf)),
            })
        res = run_bass_kernel_spmd(nc, in_maps, core_ids=list(range(n)))
        out = np.empty_like(x)
        for c in range(n):
            b, hf = c // 2, c % 2
            out[b, hf * HALF:(hf + 1) * HALF] = res.results[c]["out"]
        return out
```

```python
import numpy as np
from contextlib import ExitStack

import concourse.bass as bass
import concourse.mybir as mybir
from concourse.bass_utils import run_bass_kernel_spmd

F32 = mybir.dt.float32
BF16 = mybir.dt.bfloat16
AF = mybir.ActivationFunctionType
ALU = mybir.AluOpType

D = 1024
NCOL = 8704
C_V, C_B, C_C, C_Z = 0, 1024, 2048, 3072
C_Q, C_K, C_VV, C_ZA, C_G = 4096, 5120, 5376, 5632, 6656
EPS = 1e-6
SEMCHUNK = 16384


class Op:
    __slots__ = ("eng", "fn", "waits", "signal", "seq", "dma_key", "dma_cnt",
                 "dma_n", "clock", "sigidx", "dma_inc")


class Sched:
    ENGS = ("pe", "act", "dve", "pool", "sp")

    def __init__(self):
        self.ops = {e: [] for e in self.ENGS}
        self.know = {e: {} for e in self.ENGS}
        self.lastw = {}
        self.readers = {}
        self.dmacnt = {}

    def add(self, eng, fn, reads=(), writes=(), dma=None, dma_n=1, dma_inc=16):
        op = Op()
        op.dma_inc = dma_inc
        op.eng, op.fn, op.waits, op.signal = eng, fn, [], False
        op.dma_key, op.dma_n, op.sigidx = dma, dma_n, None
        know = self.know[eng]
        deps = []
        for r in reads:
            w = self.lastw.get(r)
            if w is not None:
                deps.append((w, True))
        for w_ in writes:
            lw = self.lastw.get(w_)
            if lw is not None:
                deps.append((lw, False))
            rd = self.readers.get(w_)
            if rd:
                for o in rd.values():
                    deps.append((o, False))
        for d, raw in deps:
            if d.dma_key is not None:
                key, val = ("dma", d.dma_key), d.dma_cnt
            else:
                if d.eng == eng and eng == "pe":
                    continue
                key, val = ("eng", d.eng), d.seq
            if know.get(key, -1) >= val:
                continue
            op.waits.append(d)
            d.signal = True
            for k2, v2 in d.clock.items():
                if know.get(k2, -1) < v2:
                    know[k2] = v2
        op.seq = len(self.ops[eng])
        op.clock = dict(know)
        if dma is not None:
            c = self.dmacnt.get(dma, 0) + dma_n
            self.dmacnt[dma] = c
            op.dma_cnt = c
            op.clock[("dma", dma)] = c
            rkey = ("dma", dma, c)
        else:
            op.dma_cnt = None
            op.clock[("eng", eng)] = op.seq
            rkey = eng
        for r in reads:
            self.readers.setdefault(r, {})[rkey] = op
        for w_ in writes:
            self.lastw[w_] = op
            self.readers[w_] = {}
        self.ops[eng].append(op)
        return op

    def emit(self, nc, es):
        esems = {}
        for e in self.ENGS:
            n = 0
            for op in self.ops[e]:
                if op.signal and op.dma_key is None:
                    op.sigidx = n
                    n += 1
            esems[e] = [es.enter_context(nc.semaphore(f"s_{e}_{i}"))
                        for i in range((n + SEMCHUNK - 1) // SEMCHUNK)]
        dsems = {k: es.enter_context(nc.semaphore("d_" + "_".join(str(x) for x in (k if isinstance(k, tuple) else (k,)))))
                 for k in self.dmacnt}

        def sem_of(d):
            if d.dma_key is not None:
                return dsems[d.dma_key], d.dma_inc * d.dma_cnt
            return esems[d.eng][d.sigidx // SEMCHUNK], d.sigidx % SEMCHUNK + 1

        def body(engobj, e):
            for op in self.ops[e]:
                waits = list(op.waits)
                attach = None
                if waits and op.fn is not None and op.dma_key is None:
                    attach = waits.pop()
                for d in waits:
                    sem, val = sem_of(d)
                    engobj.wait_ge(sem, val)
                if op.fn is None:
                    continue
                res = op.fn(engobj)
                if attach is not None:
                    sem, val = sem_of(attach)
                    res._wait_ge(sem, val)
                if op.dma_key is not None:
                    lst = res if isinstance(res, (list, tuple)) else [res]
                    assert len(lst) == op.dma_n
                    for i_ in lst:
                        i_.then_inc(dsems[op.dma_key], op.dma_inc)
                elif op.signal:
                    sem, _ = sem_of(op)
                    res.then_inc(sem, 1)

        block = es.enter_context(nc.Block())

        @block.tensor
        def _(t):
            body(t, "pe")

        @block.scalar
        def _(s):
            body(s, "act")

        @block.vector
        def _(v):
            body(v, "dve")

        @block.gpsimd
        def _(g):
            body(g, "pool")

        @block.sync
        def _(sy):
            body(sy, "sp")


def param_offsets(NL):
    o = {}
    c = 0
    for name, n in (("ng", NL * 8), ("gb", NL * 16), ("cw", NL * 24), ("qg", NL),
                    ("kg", NL), ("sk", NL * 16), ("fr", 1)):
        o[name] = c
        c += n
    o["_n"] = c
    return o


def pack_params(norm_g, gate_b, conv_w, q_norm_g, k_norm_g, sinks, fr):
    NL = norm_g.shape[0]
    o = param_offsets(NL)
    P = np.zeros((128, o["_n"]), np.float32)
    P[:, o["ng"]:o["ng"] + NL * 8] = norm_g.reshape(NL, 8, 128).transpose(2, 0, 1).reshape(128, NL * 8)
    P[:, o["gb"]:o["gb"] + NL * 16] = gate_b.reshape(NL, 16, 128).transpose(2, 0, 1).reshape(128, NL * 16)
    P[:, o["cw"]:o["cw"] + NL * 24] = conv_w.reshape(NL, 3, 8, 128).transpose(3, 0, 1, 2).reshape(128, NL * 24)
    P[:, o["qg"]:o["qg"] + NL] = np.concatenate([q_norm_g, q_norm_g], axis=1).T
    P[:, o["kg"]:o["kg"] + NL] = np.concatenate([k_norm_g, k_norm_g], axis=1).T
    P[:, o["sk"]:o["sk"] + NL * 16] = np.broadcast_to(sinks.reshape(1, NL * 16), (128, NL * 16))
    P[:, o["fr"]] = fr
    return P


def build(NL, NBLK, FR, OUT0, layer_passes, NSLOT=4, HALO_KV=True, GATE_IN_E=False, NXS=4):
    def _norm(lp):
        r = []
        for i_, p_ in enumerate(lp):
            r.append((p_[0], p_[1], p_[2] if len(p_) > 2 else ("first" if i_ == 0 else "cont")))
        return r
    layer_passes = [_norm(lp) for lp in layer_passes]
    NT = NBLK * 128
    NBMAX = max(p_[1] for lp in layer_passes for p_ in lp)
    TMAX = NBMAX * 128
    po = param_offsets(NL)
    NP = po["_n"]

    nc = bass.Bass("TRN2", target_bir_lowering=False)
    x_d = nc.dram_tensor("x", [NT, D], F32, kind="ExternalInput").ap()
    win_d = nc.dram_tensor("w_in", [NL, D, NCOL], F32, kind="ExternalInput").ap()
    wco_d = nc.dram_tensor("w_co", [NL, D, D], F32, kind="ExternalInput").ap()
    wao_d = nc.dram_tensor("w_ao", [NL, D, D], F32, kind="ExternalInput").ap()
    wo_d = nc.dram_tensor("w_o", [NL, D, D], F32, kind="ExternalInput").ap()
    par_d = nc.dram_tensor("params", [128, NP], F32, kind="ExternalInput").ap()
    out_d = nc.dram_tensor("out", [(NBLK - OUT0) * 128, D], F32, kind="ExternalOutput").ap()

    KVW = 4 * 128 + 4 * 65
    n_send = sum(1 for lp in layer_passes for p_ in lp if "send" in p_[2])
    xch = []
    for i_ in range(n_send):
        xch.append((
            nc.dram_tensor(f"xb_kv{i_}", [128, KVW], BF16, kind="Internal").ap(),
            nc.dram_tensor(f"xg_kv{i_}", [256, KVW], BF16, addr_space="Local", kind="Internal").ap(),
            nc.dram_tensor(f"xb_c{i_}", [128, 16], F32, kind="Internal").ap(),
            nc.dram_tensor(f"xg_c{i_}", [256, 16], F32, addr_space="Local", kind="Internal").ap(),
        ))
    S = Sched()
    es = ExitStack()
    E = es.enter_context
    xT = E(nc.sbuf_tensor("xT", [128, 8, NT], F32))
    hT = E(nc.sbuf_tensor("hT", [128, 8, TMAX], BF16))
    maT = E(nc.sbuf_tensor("maT", [128, 8, TMAX], BF16))
    uT = E(nc.sbuf_tensor("uT", [128, 8, TMAX], BF16))
    kT = E(nc.sbuf_tensor("kT", [128, 4, TMAX + 128], BF16))
    va = E(nc.sbuf_tensor("va", [128, NBMAX + 1, 4, 65], BF16))
    cv = E(nc.sbuf_tensor("cv", [128, TMAX + 2], F32))
    cvc = E(nc.sbuf_tensor("cvc", [128, 8, 2], F32))
    wsl = [E(nc.sbuf_tensor(f"wsl{i}", [128, 8, 256], BF16)) for i in range(NSLOT)]
    NT32 = 5
    t32 = E(nc.sbuf_tensor("t32", [128, NT32, 512], F32))
    NTB = 6
    tb = E(nc.sbuf_tensor("tb", [128, NTB, 512], BF16))
    on = [E(nc.sbuf_tensor(f"on{i}", [128, 1024], BF16)) for i in range(2)]
    den = [E(nc.sbuf_tensor(f"den{i}", [128, 4, 1], F32)) for i in range(2)]
    mpc = E(nc.sbuf_tensor("mpc", [128, 4, 128], BF16))
    mpcfr = E(nc.sbuf_tensor("mpcfr", [128, 4, 128], BF16))
    identf = E(nc.sbuf_tensor("identf", [128, 128], F32))
    identb = E(nc.sbuf_tensor("identb", [128, 128], BF16))
    onesb = E(nc.sbuf_tensor("onesb", [128, 128], BF16))
    bdb = E(nc.sbuf_tensor("bdb", [128, 128], BF16))
    xst = E(nc.sbuf_tensor("xst", [128, NXS, 1024], F32))
    par = E(nc.sbuf_tensor("par", [128, NP], F32))
    rcv = E(nc.sbuf_tensor("rcv", [128, 8, 2], F32))
    if GATE_IN_E:
        GT = max((p_[1] - (1 if (p_[2].startswith("first") and HALO_KV) else 0)) for lp in layer_passes for p_ in lp) * 128
        gbT = E(nc.sbuf_tensor("gbT", [128, 8, GT], BF16))
        hgb = E(nc.sbuf_tensor("hgb", [128, NL, 8], F32))
    NPS = 7
    psb = [E(nc.psum_tensor(f"ps{i}", [128, 512], F32)) for i in range(NPS)]
    pst = E(nc.psum_tensor("pst", [128, 8, 128], BF16))

    cnt = {"xs": 0, "ps": 0, "ps7": 0, "w": 0, "t32": 0, "tb": 0, "sE": 0, "oE": 0}

    def next_ps(full=False):
        if full:
            i = cnt["ps7"] % NPS
            cnt["ps7"] += 1
            return i
        i = cnt["ps"] % (NPS - 1)
        cnt["ps"] += 1
        return i

    def T32(i):
        return t32[:, i, :]

    def TB(i):
        return tb[:, i, :]

    S.add("sp", lambda e: e.dma_start(out=par[:], in_=par_d), writes=[("par",)], dma="par")
    S.add("pool", lambda e: e.memset(identf[:], 1.0), writes=[("identf",)])
    S.add("pool", lambda e: e.affine_select(out=identf[:], in_=identf[:], pattern=[[-1, 128]],
                                            compare_op=ALU.is_equal, fill=0.0, base=0, channel_multiplier=1),
          reads=[("identf",)], writes=[("identf",)])
    S.add("pool", lambda e: e.tensor_copy(out=identb[:], in_=identf[:]), reads=[("identf",)], writes=[("identb",)])
    S.add("pool", lambda e: e.memset(onesb[:], 1.0), writes=[("onesb",)])
    S.add("pool", lambda e: e.memset(bdb[:], 0.0), writes=[("bdb",)])
    S.add("pool", lambda e: e.memset(bdb[0:64, 0:64], 1.0), reads=[("bdb",)], writes=[("bdb",)])
    S.add("pool", lambda e: e.memset(bdb[64:128, 64:128], 1.0), reads=[("bdb",)], writes=[("bdb",)])
    S.add("pool", lambda e: e.memset(mpc[:], 1.0), writes=[("mpc",)])
    S.add("pool", lambda e: e.affine_select(out=mpc[:, 0:2, :], in_=mpc[:, 0:2, :], pattern=[[0, 2], [-1, 128]],
                                            compare_op=ALU.is_gt, fill=0.0, base=0, channel_multiplier=1),
          reads=[("mpc",)], writes=[("mpc",)])
    S.add("pool", lambda e: e.affine_select(out=mpc[:, 2:4, :], in_=mpc[:, 2:4, :], pattern=[[0, 2], [1, 128]],
                                            compare_op=ALU.is_ge, fill=0.0, base=0, channel_multiplier=-1),
          reads=[("mpc",)], writes=[("mpc",)])
    S.add("pool", lambda e: e.tensor_copy(out=mpcfr[:, 2:4, :], in_=mpc[:, 2:4, :]),
          reads=[("mpc",)], writes=[("mpcfr",)])
    S.add("pool", lambda e: e.tensor_scalar(out=mpcfr[:, 0:2, :], in0=mpc[:, 0:2, :], scalar1=par[:, po["fr"]:po["fr"] + 1],
                                            scalar2=None, op0=ALU.mult),
          reads=[("mpc",), ("par",), ("mpcfr",)], writes=[("mpcfr",)])
    S.add("dve", lambda e: e.tensor_scalar(out=par[:, po["qg"]:po["qg"] + NL], in0=par[:, po["qg"]:po["qg"] + NL],
                                           scalar1=0.125, scalar2=None, op0=ALU.mult),
          reads=[("par",)], writes=[("par2",)])
    S.add("act", lambda e: e.activation(out=par[:, po["sk"]:po["sk"] + NL * 16], in_=par[:, po["sk"]:po["sk"] + NL * 16],
                                        func=AF.Exp),
          reads=[("par",)], writes=[("par3",)])
    PARR = [("par",), ("par2",), ("par3",)]
    if GATE_IN_E:
        S.add("dve", lambda e: e.tensor_scalar(
            out=hgb[:], in0=par[:, po["gb"]:po["gb"] + NL * 16].rearrange("p (l j) -> p l j", j=16)[:, :, 8:16],
            scalar1=0.5, scalar2=None, op0=ALU.mult), reads=[("par",)], writes=[("hgb",)])
    S.add("pool", lambda e: e.memset(va[:], 1.0), writes=[("va", i) for i in range(NBMAX + 1)])
    S.add("pool", lambda e: e.memset(cvc[:], 0.0), writes=[("cvc", j) for j in range(8)])

    def stage(i):
        return xst[:, i, :]

    def load_x_blocks(blks, wide=False):
        for blk in blks:
            nst = NXS + (2 if wide else 0)
            si = cnt["xs"] % nst
            cnt["xs"] += 1
            if si < NXS:
                sap, skeys = stage(si), [("xst", si, 0), ("xst", si, 1)]
            else:
                j_ = si - NXS
                sap = t32[:, 2 * j_:2 * j_ + 2, :].rearrange("p a b -> p (a b)")
                skeys = [("t32", 2 * j_), ("t32", 2 * j_ + 1)]
            S.add("sp", lambda e, blk=blk, sap=sap: e.dma_start(out=sap, in_=x_d[blk * 128:(blk + 1) * 128, :]),
                  writes=skeys, dma=("st", si))
            for hh in range(2):
                pi = next_ps()
                for kk in range(4):
                    k = hh * 4 + kk
                    S.add("pe", lambda e, pi=pi, kk=kk, k=k, sap=sap: e.transpose(
                        psb[pi][:, kk * 128:(kk + 1) * 128], sap[:, k * 128:(k + 1) * 128], identf[:]),
                        reads=skeys + [("identf",)], writes=[("ps", pi)])
                if hh == 0:
                    fn = lambda e, pi=pi, hh=hh, blk=blk: e.activation(
                        out=xT[:, hh * 4:hh * 4 + 4, blk * 128:(blk + 1) * 128],
                        in_=psb[pi][:].rearrange("p (a b) -> p a b", a=4), func=AF.Copy)
                    eng = "act"
                else:
                    fn = lambda e, pi=pi, hh=hh, blk=blk: e.tensor_copy(
                        out=xT[:, hh * 4:hh * 4 + 4, blk * 128:(blk + 1) * 128],
                        in_=psb[pi][:].rearrange("p (a b) -> p a b", a=4))
                    eng = "dve"
                S.add(eng, fn, reads=[("ps", pi)], writes=[("xT", k, blk) for k in range(hh * 4, hh * 4 + 4)])

    def store_blocks(blks):
        for blk in blks:
            si = cnt["xs"] % NXS
            cnt["xs"] += 1
            for hh in range(2):
                pi = next_ps()
                for kk in range(4):
                    k = hh * 4 + kk
                    S.add("pe", lambda e, pi=pi, kk=kk, k=k, blk=blk: e.transpose(
                        psb[pi][:, kk * 128:(kk + 1) * 128], xT[:, k, blk * 128:(blk + 1) * 128], identf[:]),
                        reads=[("xT", k, blk), ("identf",)], writes=[("ps", pi)])
                if hh == 0:
                    S.add("act", lambda e, pi=pi, si=si: e.activation(out=xst[:, si, 0:512], in_=psb[pi][:], func=AF.Copy),
                          reads=[("ps", pi)], writes=[("xst", si, 0)])
                else:
                    S.add("dve", lambda e, pi=pi, si=si: e.tensor_copy(out=xst[:, si, 512:1024], in_=psb[pi][:]),
                          reads=[("ps", pi)], writes=[("xst", si, 1)])
            ob = blk - OUT0
            S.add("sp", lambda e, si=si, ob=ob: e.dma_start(out=out_d[ob * 128:(ob + 1) * 128, :], in_=stage(si)),
                  reads=[("xst", si, 0), ("xst", si, 1)], writes=[("out", ob)], dma=("o", si))
            outs.append(("out", ob))

    outs = []
    first_b0, first_nb = layer_passes[0][0][0], layer_passes[0][0][1]
    early = list(range(first_b0, first_b0 + first_nb))
    late = [b_ for b_ in range(NBLK) if b_ not in early]
    start_gen = None

    def load_w(pieces):
        si = cnt["w"] % NSLOT
        cnt["w"] += 1

        def fn(e, si=si, pieces=pieces):
            r = []
            for (c0, n, src) in pieces:
                r.append(e.dma_start(out=wsl[si][:, :, c0:c0 + n], in_=src.rearrange("(k p) c -> p k c", p=128)))
            return r
        gate = []
        if 2 <= cnt["w"] - 1 < NSLOT:
            gate = [("xT", 7, early[-1])]
        S.add("pool", fn, reads=gate, writes=[("w", si)], dma=("w", si), dma_n=len(pieces))
        return si

    def tiles_of(nb):
        nt = (nb + 3) // 4
        base, extra = nb // nt, nb % nt
        r = []
        b = 0
        for i in range(nt):
            n = base + (1 if i < extra else 0)
            r.append((b, n))
            b += n
        return r

    def proj(si, ci, src, srckey, tb0, tn, full=False, bank=None):
        pi = bank if bank is not None else next_ps(full)
        n = tn * 128
        for k in range(8):
            S.add("pe", lambda e, pi=pi, si=si, ci=ci, k=k, tb0=tb0, n=n, src=src: e.matmul(
                psb[pi][:, 0:n], lhsT=wsl[si][:, k, ci * 128:(ci + 1) * 128],
                rhs=src[:, k, tb0 * 128:tb0 * 128 + n], start=(k == 0), stop=(k == 7)),
                reads=[("w", si)] + [(srckey, k, tb0 + j) for j in range(tn)], writes=[("ps", pi)])
        return pi

    def pass_tiles(NB, halo):
        if halo:
            return [(0, 1, True)] + [(1 + b_, n_, False) for (b_, n_) in tiles_of(NB - 1)]
        return [(b_, n_, False) for (b_, n_) in tiles_of(NB)]

    def phase0_steps(l, B0, NB, halo, which="all"):
        pi = NPS - 1
        for (tb0, tn, _hl) in pass_tiles(NB, halo):
            if (which == "halo" and not _hl) or (which == "nonhalo" and _hl):
                continue
            n = tn * 128
            g0 = (B0 + tb0) * 128
            for k in range(8):
                bi = cnt["tb"] % 2
                cnt["tb"] += 1
                S.add("act", lambda e, bi=bi, k=k, g0=g0, n=n: e.activation(
                    out=TB(bi)[:, 0:n], in_=xT[:, k, g0:g0 + n], func=AF.Square),
                    reads=[("xT", k, B0 + tb0 + j) for j in range(tn)], writes=[("tb", bi)])
                S.add("pe", lambda e, pi=pi, bi=bi, k=k, n=n: e.matmul(
                    psb[pi][:, 0:n], lhsT=onesb[:], rhs=TB(bi)[:, 0:n], start=(k == 0), stop=(k == 7)),
                    reads=[("tb", bi), ("onesb",)], writes=[("ps", pi)])
                yield
            ri = 4
            S.add("act", lambda e, pi=pi, n=n, ri=ri: e.activation(
                out=T32(ri)[:, 0:n], in_=psb[pi][:, 0:n], func=AF.Ln, bias=EPS, scale=1.0 / D),
                reads=[("ps", pi)], writes=[("t32", ri)])
            S.add("act", lambda e, n=n, ri=ri: e.activation(
                out=T32(ri)[:, 0:n], in_=T32(ri)[:, 0:n], func=AF.Exp, scale=-0.5),
                reads=[("t32", ri)], writes=[("t32", ri)])
            for k in range(8):
                S.add("dve", lambda e, k=k, g0=g0, n=n, tb0=tb0, ri=ri, l=l: e.scalar_tensor_tensor(
                    out=hT[:, k, tb0 * 128:tb0 * 128 + n], in0=xT[:, k, g0:g0 + n],
                    scalar=par[:, po["ng"] + l * 8 + k:po["ng"] + l * 8 + k + 1], in1=T32(ri)[:, 0:n],
                    op0=ALU.mult, op1=ALU.mult),
                    reads=[("xT", k, B0 + tb0 + j) for j in range(tn)] + [("t32", ri)] + PARR,
                    writes=[("hT", k, tb0 + j) for j in range(tn)])
            yield

    def advance(gen, nsteps):
        if gen is None:
            return None
        for _ in range(nsteps):
            try:
                next(gen)
            except StopIteration:
                return None
        return gen

    hoisted_next_layer = {}
    send_state = {"i": 0, "last": None}
    for l in range(NL):
        passes = layer_passes[l]
        for pidx, (B0, NB, mode) in enumerate(passes):
            T = NB * 128
            first_pass = mode.startswith("first")
            halo = 1 if (first_pass and HALO_KV) else 0
            tiles = pass_tiles(NB, halo)
            if mode == "remote":
                xb_kv, xg_kv, xb_c, xg_c = send_state["last"]
                S.add("sp", lambda e, xg_kv=xg_kv: e.dma_start(
                    out=kT[:, :, 0:128], in_=xg_kv[0:128, 0:512].rearrange("p (g t) -> p g t", g=4)),
                    reads=[("xg",)], writes=[("kT", c, h_, 0) for c in range(4) for h_ in range(2)], dma=("rx", 0))
                S.add("sp", lambda e, xg_kv=xg_kv: e.dma_start(
                    out=va[:, 0, :, :], in_=xg_kv[0:128, 512:KVW].rearrange("p (g d) -> p g d", g=4)),
                    reads=[("xg",)], writes=[("va", 0)], dma=("rx", 1))
                S.add("sp", lambda e, xg_c=xg_c: e.dma_start(
                    out=rcv[:].rearrange("p a b -> p (a b)"), in_=xg_c[0:128, :]),
                    reads=[("xgc",)], writes=[("rcv",)], dma=("rx", 2))
                S.add("dve", lambda e: e.tensor_scalar(out=cvc[:], in0=rcv[:], scalar1=par[:, po["fr"]:po["fr"] + 1],
                                                       scalar2=None, op0=ALU.mult),
                      reads=[("rcv",), ("par",)], writes=[("cvc", j) for j in range(8)])
            if first_pass:
                S.add("pool", lambda e: e.memset(kT[:, :, 0:128], 0.0), writes=[("kT", c, h_, 0) for c in range(4) for h_ in range(2)])
                S.add("pool", lambda e: e.memset(va[:, 0, :, 0:64], 0.0), writes=[("va", 0)])
                S.add("pool", lambda e: e.memset(cvc[:], 0.0), writes=[("cvc", j) for j in range(8)])

            if l == 0 and pidx == 0:
                g_ = phase0_steps(l, B0, NB, halo)
                for (tb0_, tn_, _h) in tiles:
                    load_x_blocks(list(range(B0 + tb0_, B0 + tb0_ + tn_)), wide=True)
                    g_ = advance(g_, 9)
                advance(g_, 10 ** 6)
            elif pidx == 0 and hoisted_next_layer.get(l, None) == "nonhalo":
                advance(phase0_steps(l, B0, NB, halo, "halo"), 10 ** 6)
            elif pidx == 0 and not hoisted_next_layer.get(l, False):
                advance(phase0_steps(l, B0, NB, halo), 10 ** 6)
            late_q = list(late) if (l == 0 and pidx == 0) else []

            for j in range(8):
                sA = load_w([(0, 128, win_d[l, :, C_V + j * 128:C_V + (j + 1) * 128]),
                             (128, 128, win_d[l, :, C_C + j * 128:C_C + (j + 1) * 128])])
                sB = load_w([(0, 128, win_d[l, :, C_B + j * 128:C_B + (j + 1) * 128]),
                             (128, 128, win_d[l, :, C_Z + j * 128:C_Z + (j + 1) * 128])])
                S.add("dve", lambda e, j=j: e.tensor_copy(out=cv[:, 0:2], in_=cvc[:, j, :]),
                      reads=[("cvc", j)], writes=[("cvh",)])
                for (tb0, tn, hl) in tiles:
                    n = tn * 128
                    t0 = tb0 * 128
                    pv = proj(sA, 0, hT, "hT", tb0, tn)
                    pc = proj(sA, 1, hT, "hT", tb0, tn)
                    vi = cnt["t32"] % 2
                    cnt["t32"] += 1
                    zi = 2 + vi
                    yi = 4
                    S.add("act", lambda e, pv=pv, vi=vi, n=n: e.activation(out=T32(vi)[:, 0:n], in_=psb[pv][:, 0:n], func=AF.Copy),
                          reads=[("ps", pv)], writes=[("t32", vi)])
                    S.add("dve", lambda e, pc=pc, vi=vi, n=n, t0=t0: e.tensor_tensor(
                        out=cv[:, 2 + t0:2 + t0 + n], in0=psb[pc][:, 0:n], in1=T32(vi)[:, 0:n], op=ALU.mult),
                        reads=[("ps", pc), ("t32", vi)], writes=[("cvb", tb0)])
                    if hl:
                        continue
                    pb = proj(sB, 0, hT, "hT", tb0, tn)
                    pz = proj(sB, 1, hT, "hT", tb0, tn)
                    S.add("act", lambda e, pz=pz, zi=zi, n=n: e.activation(out=T32(zi)[:, 0:n], in_=psb[pz][:, 0:n], func=AF.Silu),
                          reads=[("ps", pz)], writes=[("t32", zi)])
                    rd_cv = [("cvb", tb0), ("cvh",)] + ([("cvb", tiles[[t[0] for t in tiles].index(tb0) - 1][0])] if tb0 > 0 else [])
                    cwo = po["cw"] + l * 24
                    S.add("dve", lambda e, n=n, t0=t0, yi=yi, j=j, cwo=cwo: e.tensor_scalar(
                        out=T32(yi)[:, 0:n], in0=cv[:, t0:t0 + n], scalar1=par[:, cwo + j:cwo + j + 1],
                        scalar2=None, op0=ALU.mult),
                        reads=rd_cv + PARR, writes=[("t32", yi)])
                    for kk in (1, 2):
                        S.add("dve", lambda e, n=n, t0=t0, yi=yi, j=j, cwo=cwo, kk=kk: e.scalar_tensor_tensor(
                            out=T32(yi)[:, 0:n], in0=cv[:, t0 + kk:t0 + kk + n],
                            scalar=par[:, cwo + kk * 8 + j:cwo + kk * 8 + j + 1], in1=T32(yi)[:, 0:n],
                            op0=ALU.mult, op1=ALU.add),
                            reads=rd_cv + [("t32", yi)] + PARR, writes=[("t32", yi)])
                    S.add("dve", lambda e, n=n, yi=yi, pb=pb: e.tensor_tensor(
                        out=T32(yi)[:, 0:n], in0=psb[pb][:, 0:n], in1=T32(yi)[:, 0:n], op=ALU.mult),
                        reads=[("ps", pb), ("t32", yi)], writes=[("t32", yi)])
                    S.add("dve", lambda e, n=n, yi=yi, zi=zi, j=j, t0=t0: e.tensor_tensor(
                        out=uT[:, j, t0:t0 + n], in0=T32(yi)[:, 0:n], in1=T32(zi)[:, 0:n], op=ALU.mult),
                        reads=[("t32", yi), ("t32", zi)], writes=[("uT", j, tb0 + jj) for jj in range(tn)])
                lt = tiles[-1][0]
                S.add("dve", lambda e, j=j, T=T: e.tensor_copy(out=cvc[:, j, :], in_=cv[:, T:T + 2]),
                      reads=[("cvb", lt)], writes=[("cvc", j)])
                if late_q:
                    load_x_blocks([late_q.pop(0)])
            if late_q:
                load_x_blocks(late_q)
                late_q = []

            for ep in range(4):
                s1 = load_w([(0, 256, wco_d[l, :, ep * 256:(ep + 1) * 256])])
                s2 = load_w([(0, 256, win_d[l, :, C_G + ep * 256:C_G + (ep + 1) * 256])])
                for ee in range(2):
                    e_ = ep * 2 + ee
                    for (tb0, tn, hl) in tiles:
                        if hl:
                            continue
                        n = tn * 128
                        t0 = tb0 * 128
                        pga = proj(s2, ee, hT, "hT", tb0, tn)
                        pya = proj(s1, ee, uT, "uT", tb0, tn)
                        gi = cnt["t32"] % 2
                        cnt["t32"] += 1
                        gbo = po["gb"] + l * 16 + e_
                        S.add("act", lambda e, pga=pga, gi=gi, n=n, gbo=gbo: e.activation(
                            out=T32(gi)[:, 0:n], in_=psb[pga][:, 0:n], func=AF.Sigmoid, bias=par[:, gbo:gbo + 1], scale=1.0),
                            reads=[("ps", pga)] + PARR, writes=[("t32", gi)])
                        S.add("dve", lambda e, pya=pya, gi=gi, n=n, e_=e_, t0=t0: e.tensor_tensor(
                            out=maT[:, e_, t0:t0 + n], in0=psb[pya][:, 0:n], in1=T32(gi)[:, 0:n], op=ALU.mult),
                            reads=[("ps", pya), ("t32", gi)], writes=[("maT", e_, tb0 + jj) for jj in range(tn)])

            units = []
            for cp in range(4):
                units.append(("q", cp))
            units.append(("k", 0))
            pend = None

            def finish_qk(pd):
                pq, bi, kind, c, tb0, tn = pd
                n = tn * 128
                t0 = tb0 * 128
                pss = next_ps(True)
                S.add("pe", lambda e, pss=pss, bi=bi, n=n: e.matmul(
                    psb[pss][:, 0:n], lhsT=bdb[:], rhs=TB(bi)[:, 0:n], start=True, stop=True),
                    reads=[("tb", bi), ("bdb",)], writes=[("ps", pss)])
                ri = cnt["t32"] % 2
                cnt["t32"] += 1
                S.add("act", lambda e, pss=pss, ri=ri, n=n: e.activation(
                    out=T32(ri)[:, 0:n], in_=psb[pss][:, 0:n], func=AF.Ln, bias=EPS, scale=1.0 / 64),
                    reads=[("ps", pss)], writes=[("t32", ri)])
                S.add("act", lambda e, ri=ri, n=n: e.activation(
                    out=T32(ri)[:, 0:n], in_=T32(ri)[:, 0:n], func=AF.Exp, scale=-0.5),
                    reads=[("t32", ri)], writes=[("t32", ri)])
                if kind == "q":
                    go = po["qg"] + l
                    S.add("dve", lambda e, pq=pq, ri=ri, n=n, c=c, t0=t0, go=go: e.scalar_tensor_tensor(
                        out=uT[:, c, t0:t0 + n], in0=psb[pq][:, 0:n], scalar=par[:, go:go + 1], in1=T32(ri)[:, 0:n],
                        op0=ALU.mult, op1=ALU.mult),
                        reads=[("ps", pq), ("t32", ri)] + PARR, writes=[("uT", c, tb0 + jj) for jj in range(tn)])
                else:
                    go = po["kg"] + l
                    for h_ in range(2):
                        p0, p1 = h_ * 64, (h_ + 1) * 64
                        S.add("dve", lambda e, pq=pq, ri=ri, n=n, c=c, t0=t0, go=go, p0=p0, p1=p1, h_=h_: e.scalar_tensor_tensor(
                            out=kT[p0:p1, 2 * c + h_, 128 + t0:128 + t0 + n], in0=psb[pq][p0:p1, 0:n],
                            scalar=par[p0:p1, go:go + 1], in1=T32(ri)[p0:p1, 0:n], op0=ALU.mult, op1=ALU.mult),
                            reads=[("ps", pq), ("t32", ri)] + PARR,
                            writes=[("kT", 2 * c + h_, h_, 1 + tb0 + jj) for jj in range(tn)])

            for (kind, idx) in units:
                if kind == "q":
                    si = load_w([(0, 256, win_d[l, :, C_Q + idx * 256:C_Q + (idx + 1) * 256])])
                else:
                    si = load_w([(0, 256, win_d[l, :, C_K:C_K + 256])])
                for cc in range(2):
                    c = idx * 2 + cc
                    for (tb0, tn, hl) in tiles:
                        if hl and kind == "q":
                            continue
                        n = tn * 128
                        pq = proj(si, cc, hT, "hT", tb0, tn, True)
                        bi = 2 + cnt["tb"] % 2
                        cnt["tb"] += 1
                        S.add("act", lambda e, pq=pq, bi=bi, n=n: e.activation(
                            out=TB(bi)[:, 0:n], in_=psb[pq][:, 0:n], func=AF.Square),
                            reads=[("ps", pq)], writes=[("tb", bi)])
                        if pend is not None:
                            finish_qk(pend)
                        pend = (pq, bi, kind, c, tb0, tn)
            finish_qk(pend)
            pend = None
            for g in range(4):
                own = g % 2
                oth = 1 - own
                S.add("sp", lambda e, g=g, own=own, oth=oth, T=T: e.dma_start(
                    out=kT[oth * 64:(oth + 1) * 64, g, 128:128 + T], in_=kT[own * 64:(own + 1) * 64, g, 128:128 + T]),
                    reads=[("kT", g, own, 1 + jj) for jj in range(NB)], writes=[("kT", g, oth, 1 + jj) for jj in range(NB)],
                    dma=("kd", g))

            sv = load_w([(0, 256, win_d[l, :, C_VV:C_VV + 256])])

            def vproj(b, pi):
                for k in range(8):
                    S.add("pe", lambda e, pi=pi, k=k, b=b, sv=sv: e.matmul(
                        psb[pi][:, 0:256], lhsT=hT[:, k, b * 128:(b + 1) * 128], rhs=wsl[sv][:, k, 0:256],
                        start=(k == 0), stop=(k == 7)),
                        reads=[("w", sv), ("hT", k, b)], writes=[("ps", pi)])
                S.add("act", lambda e, pi=pi, b=b: e.activation(
                    out=va[:, b + 1, :, 0:64], in_=psb[pi][:, 0:256].rearrange("p (g d) -> p g d", g=4), func=AF.Copy),
                    reads=[("ps", pi)], writes=[("va", b + 1)])

            DLATE = 3 if NB - halo >= 5 else 0
            for b in range(NB - DLATE):
                vproj(b, next_ps())

            def send_halo():
                xb_kv, xg_kv, xb_c, xg_c = xch[send_state["i"]]
                send_state["i"] += 1
                send_state["last"] = (xb_kv, xg_kv, xb_c, xg_c)
                sidx = send_state["i"]
                S.add("sp", lambda e, xb_kv=xb_kv, NB=NB: e.dma_start(
                    out=xb_kv[:, 0:512].rearrange("p (g t) -> p g t", g=4), in_=kT[:, :, NB * 128:(NB + 1) * 128]),
                    reads=[("kT", c, h_, NB) for c in range(4) for h_ in range(2)], writes=[("xb",)], dma=("tx", 0))
                S.add("sp", lambda e, xb_kv=xb_kv, NB=NB: e.dma_start(
                    out=xb_kv[:, 512:KVW].rearrange("p (g d) -> p g d", g=4), in_=va[:, NB, :, :]),
                    reads=[("va", NB)], writes=[("xb2",)], dma=("tx", 1))
                S.add("sp", lambda e, xb_c=xb_c: e.dma_start(out=xb_c, in_=cvc[:].rearrange("p a b -> p (a b)")),
                      reads=[("cvc", j) for j in range(8)], writes=[("xbc",)], dma=("tx", 2))
                RG = [[0, 1], [2, 3], [4, 5], [6, 7]]
                S.add("pool", lambda e, xb_kv=xb_kv, xg_kv=xg_kv: e.collective_compute(
                    "AllGather", ALU.bypass, replica_groups=RG, ins=[xb_kv], outs=[xg_kv]),
                    reads=[("xb",), ("xb2",)], writes=[("xg",)], dma=("cc", sidx, 0), dma_inc=1)
                S.add("pool", lambda e, xb_c=xb_c, xg_c=xg_c: e.collective_compute(
                    "AllGather", ALU.bypass, replica_groups=RG, ins=[xb_c], outs=[xg_c]),
                    reads=[("xbc",)], writes=[("xgc",)], dma=("cc", sidx, 1), dma_inc=1)

            def load_za(cp):
                return load_w([(0, 256, win_d[l, :, C_ZA + cp * 256:C_ZA + (cp + 1) * 256])])
            if GATE_IN_E:
                gb_slots = [load_w([(0, 256, win_d[l, :, C_G + D + ep * 256:C_G + D + (ep + 1) * 256])]) for ep in range(4)]
                za_slots = []
                ntl = sum(1 for t_ in tiles if not t_[2])
                gsteps = [(ep, ee, tb0, tn) for ep in range(4) for ee in range(2) for (tb0, tn, hl) in tiles if not hl]
                gdone = {"n": 0}

                gcur = {"st": None, "half": 0}

                def gate_half():
                    if gcur["st"] is None:
                        gcur["st"] = gsteps.pop(0)
                        gcur["half"] = 0
                    ep, ee, tb0, tn = gcur["st"]
                    n = tn * 128
                    e_ = ep * 2 + ee
                    g0_ = (tb0 - halo) * 128
                    pgb = NPS - 1
                    si = gb_slots[ep]
                    for k in range(4 * gcur["half"], 4 * gcur["half"] + 4):
                        S.add("pe", lambda e, pgb=pgb, si=si, ee=ee, k=k, tb0=tb0, n=n: e.matmul(
                            psb[pgb][:, 0:n], lhsT=wsl[si][:, k, ee * 128:(ee + 1) * 128],
                            rhs=hT[:, k, tb0 * 128:tb0 * 128 + n], start=(k == 0), stop=(k == 7)),
                            reads=[("w", si)] + [("hT", k, tb0 + j) for j in range(tn)], writes=[("ps", pgb)])
                    if gcur["half"] == 0:
                        gcur["half"] = 1
                        return
                    S.add("act", lambda e, pgb=pgb, n=n, e_=e_, g0_=g0_, l=l: e.activation(
                        out=gbT[:, e_, g0_:g0_ + n], in_=psb[pgb][:, 0:n], func=AF.Tanh,
                        bias=hgb[:, l, e_:e_ + 1], scale=0.5),
                        reads=[("ps", pgb), ("hgb",)], writes=[("gbT", e_, tb0 + jj) for jj in range(tn)])
                    gcur["st"] = None
                    gdone["n"] += 1
                    if gdone["n"] % (2 * ntl) == 0 and len(za_slots) < 4:
                        za_slots.append(load_za(len(za_slots)))

                def gate_pending():
                    return bool(gsteps) or gcur["st"] is not None
            else:
                za_slots = [load_za(cp) for cp in range(min(4, NSLOT))]
            def qk(b, g):
                bx, by = (cnt["sE"] % 2) * 2, (cnt["sE"] % 2) * 2 + 1
                cnt["sE"] += 1
                for (slot, coff) in ((b, 0), (b + 1, 256)):
                    for (hf, pi) in ((0, bx), (1, by)):
                        S.add("pe", lambda e, pi=pi, slot=slot, coff=coff, hf=hf, g=g, b=b: e.matmul(
                            psb[pi][:, coff:coff + 256].rearrange("p (a t) -> p a t", a=2),
                            lhsT=kT[hf * 64:(hf + 1) * 64, g, slot * 128:(slot + 1) * 128],
                            rhs=uT[hf * 64:(hf + 1) * 64, 2 * g:2 * g + 2, b * 128:(b + 1) * 128],
                            start=True, stop=True, tile_position=(hf * 64, 0)),
                            reads=[("kT", g, hf, slot), ("uT", 2 * g, b), ("uT", 2 * g + 1, b)], writes=[("ps", pi)])
                return (bx, by)

            def expmask(b, g, bx, by, u):
                gblk = B0 + b
                pt = (2 * (u % 3), 2 * (u % 3) + 1)
                mk, mkey = (mpcfr, ("mpcfr",)) if gblk == FR else (mpc, ("mpc",))
                for (pi, ti, meng) in ((bx, pt[0], "pool"), (by, pt[1], "dve")):
                    S.add("act", lambda e, pi=pi, ti=ti: e.activation(out=TB(ti), in_=psb[pi][:], func=AF.Exp),
                          reads=[("ps", pi)], writes=[("tb", ti)])
                    if meng == "pool" and gblk != FR:
                        tv = TB(ti).rearrange("p (c a t) -> p c a t", c=2, a=2)
                        S.add("pool", lambda e, tv=tv: e.affine_select(
                            out=tv[:, 0, :, :], in_=tv[:, 0, :, :], pattern=[[0, 2], [-1, 128]],
                            compare_op=ALU.is_gt, fill=0.0, base=0, channel_multiplier=1),
                            reads=[("tb", ti)], writes=[("tb", ti)])
                        S.add("pool", lambda e, tv=tv: e.affine_select(
                            out=tv[:, 1, :, :], in_=tv[:, 1, :, :], pattern=[[0, 2], [1, 128]],
                            compare_op=ALU.is_ge, fill=0.0, base=0, channel_multiplier=-1),
                            reads=[("tb", ti)], writes=[("tb", ti)])
                        continue
                    S.add(meng, lambda e, ti=ti, mk=mk: e.tensor_tensor(
                        out=TB(ti), in0=TB(ti), in1=mk[:].rearrange("p a t -> p (a t)"), op=ALU.mult),
                        reads=[("tb", ti), mkey], writes=[("tb", ti)])

            def softmax_pv(b, g, bx, by, u):
                pt = (2 * (u % 3), 2 * (u % 3) + 1)
                po_ = 4 + cnt["oE"] % (2 if GATE_IN_E else 3)
                cnt["oE"] += 1
                ov = psb[po_][:, 0:260].rearrange("p (i d) -> p i d", d=65)
                for a_ in range(2):
                    for hf in range(2):
                        i = 2 * a_ + hf
                        ti = pt[hf]
                        for (coff, slot, st, sp2) in ((0, b, True, False), (256, b + 1, False, True)):
                            S.add("pe", lambda e, ti=ti, slot=slot, st=st, sp2=sp2, i=i, g=g, ov=ov, coff=coff, a_=a_: e.matmul(
                                ov[:, i, :], lhsT=TB(ti)[:, coff + a_ * 128:coff + (a_ + 1) * 128], rhs=va[:, slot, g, :],
                                start=st, stop=sp2),
                                reads=[("tb", ti), ("va", slot)], writes=[("ps", po_)])
                di = u % 2
                oi = b % 2
                sko = po["sk"] + l * 16 + 4 * g
                S.add("dve", lambda e, di=di, ov=ov, sko=sko: e.tensor_tensor(
                    out=den[di][:], in0=ov[:, :, 64:65], in1=par[:, sko:sko + 4].rearrange("p (a b) -> p a b", b=1), op=ALU.add),
                    reads=[("ps", po_)] + PARR, writes=[("den", di)])
                S.add("dve", lambda e, di=di: e.reciprocal(out=den[di][:], in_=den[di][:]),
                      reads=[("den", di)], writes=[("den", di)])
                S.add("dve", lambda e, di=di, ov=ov, oi=oi, g=g: e.tensor_tensor(
                    out=on[oi][:, g * 256:(g + 1) * 256].rearrange("p (i d) -> p i d", d=64), in0=ov[:, :, 0:64],
                    in1=den[di][:].broadcast_to([128, 4, 64]), op=ALU.mult),
                    reads=[("ps", po_), ("den", di)], writes=[("on", oi, g)])

            def finish_block(b):
                oi = b % 2
                for c in range(8):
                    S.add("pe", lambda e, c=c, oi=oi: e.transpose(pst[:, c, :], on[oi][:, c * 128:(c + 1) * 128], identb[:]),
                          reads=[("on", oi, c // 2), ("identb",)], writes=[("pst",)])
                S.add("act", lambda e, b=b: e.activation(out=uT[:, :, b * 128:(b + 1) * 128], in_=pst[:], func=AF.Copy),
                      reads=[("pst",)], writes=[("uT", c, b) for c in range(8)])

            seq = [(b, g) for b in range(halo, NB) for g in range(4)]
            units = {}
            nun = len(seq)
            for it in range(nun + 3):
                if it < nun:
                    b, g = seq[it]
                    units[it] = (b, g) + qk(b, g)
                if 0 <= it - 1 < nun:
                    b1, g1, x1, y1 = units[it - 1]
                    expmask(b1, g1, x1, y1, it - 1)
                if it == 2:
                    for i_, b in enumerate(range(NB - DLATE, NB)):
                        vproj(b, 4 + i_ % (2 if GATE_IN_E else 3))
                    if "send" in mode:
                        send_halo()
                    if GATE_IN_E:
                        za_slots.append(load_za(0))
                if 0 <= it - 3 < nun:
                    b2, g2, x2, y2 = units[it - 3]
                    softmax_pv(b2, g2, x2, y2, it - 3)
                    if g2 == 2 and b2 > halo:
                        finish_block(b2 - 1)
                if GATE_IN_E and it >= 1 and gate_pending():
                    gate_half()
            finish_block(NB - 1)
            if GATE_IN_E:
                while gate_pending():
                    gate_half()
                while len(za_slots) < 4:
                    za_slots.append(load_za(len(za_slots)))
            nxt_mode = passes[pidx + 1][2] if pidx + 1 < len(passes) else None
            if nxt_mode == "cont":
                S.add("pool", lambda e, NB=NB: e.tensor_copy(out=kT[:, :, 0:128], in_=kT[:, :, NB * 128:(NB + 1) * 128]),
                      reads=[("kT", c, h_, NB) for c in range(4) for h_ in range(2)],
                      writes=[("kT", c, h_, 0) for c in range(4) for h_ in range(2)])
                S.add("pool", lambda e, NB=NB: e.tensor_copy(out=va[:, 0, :, :], in_=va[:, NB, :, :]),
                      reads=[("va", NB)], writes=[("va", 0)])

            for cp in range(4):
                if cp < len(za_slots):
                    si = za_slots[cp]
                else:
                    si = load_za(cp)
                for cc in range(2):
                    c = cp * 2 + cc
                    for (tb0, tn, hl) in tiles:
                        if hl:
                            continue
                        n = tn * 128
                        t0 = tb0 * 128
                        pz = proj(si, cc, hT, "hT", tb0, tn)
                        zi = cnt["t32"] % 2
                        cnt["t32"] += 1
                        S.add("act", lambda e, pz=pz, zi=zi, n=n: e.activation(out=T32(zi)[:, 0:n], in_=psb[pz][:, 0:n], func=AF.Silu),
                              reads=[("ps", pz)], writes=[("t32", zi)])
                        S.add("dve", lambda e, zi=zi, n=n, c=c, t0=t0: e.tensor_tensor(
                            out=uT[:, c, t0:t0 + n], in0=uT[:, c, t0:t0 + n], in1=T32(zi)[:, 0:n], op=ALU.mult),
                            reads=[("t32", zi)] + [("uT", c, tb0 + jj) for jj in range(tn)],
                            writes=[("uT", c, tb0 + jj) for jj in range(tn)])

            if GATE_IN_E:
                for ep in range(4):
                    s1 = load_w([(0, 256, wao_d[l, :, ep * 256:(ep + 1) * 256])])
                    for ee in range(2):
                        e_ = ep * 2 + ee
                        for (tb0, tn, hl) in tiles:
                            if hl:
                                continue
                            n = tn * 128
                            t0 = tb0 * 128
                            g0_ = (tb0 - halo) * 128
                            pyb = proj(s1, ee, uT, "uT", tb0, tn)
                            ti = 2 + cnt["t32"] % 2
                            cnt["t32"] += 1
                            S.add("dve", lambda e, pyb=pyb, ti=ti, n=n, e_=e_, g0_=g0_: e.scalar_tensor_tensor(
                                out=T32(ti)[:, 0:n], in0=gbT[:, e_, g0_:g0_ + n], scalar=1.0, in1=psb[pyb][:, 0:n],
                                op0=ALU.add, op1=ALU.mult),
                                reads=[("ps", pyb)] + [("gbT", e_, tb0 + jj) for jj in range(tn)], writes=[("t32", ti)])
                            S.add("dve", lambda e, ti=ti, n=n, e_=e_, t0=t0: e.scalar_tensor_tensor(
                                out=maT[:, e_, t0:t0 + n], in0=T32(ti)[:, 0:n], scalar=0.5, in1=maT[:, e_, t0:t0 + n],
                                op0=ALU.mult, op1=ALU.add),
                                reads=[("t32", ti)] + [("maT", e_, tb0 + jj) for jj in range(tn)],
                                writes=[("maT", e_, tb0 + jj) for jj in range(tn)])
            else:
                for ep in range(4):
                    s1 = load_w([(0, 256, wao_d[l, :, ep * 256:(ep + 1) * 256])])
                    s2 = load_w([(0, 256, win_d[l, :, C_G + D + ep * 256:C_G + D + (ep + 1) * 256])])
                    for ee in range(2):
                        e_ = ep * 2 + ee
                        for (tb0, tn, hl) in tiles:
                            if hl:
                                continue
                            n = tn * 128
                            t0 = tb0 * 128
                            pgb = proj(s2, ee, hT, "hT", tb0, tn)
                            pyb = proj(s1, ee, uT, "uT", tb0, tn)
                            gi = cnt["t32"] % 2
                            cnt["t32"] += 1
                            ti = 2 + gi
                            gbo = po["gb"] + l * 16 + 8 + e_
                            S.add("act", lambda e, pgb=pgb, gi=gi, n=n, gbo=gbo: e.activation(
                                out=T32(gi)[:, 0:n], in_=psb[pgb][:, 0:n], func=AF.Sigmoid, bias=par[:, gbo:gbo + 1], scale=1.0),
                                reads=[("ps", pgb)] + PARR, writes=[("t32", gi)])
                            S.add("dve", lambda e, pyb=pyb, gi=gi, ti=ti, n=n: e.tensor_tensor(
                                out=T32(ti)[:, 0:n], in0=psb[pyb][:, 0:n], in1=T32(gi)[:, 0:n], op=ALU.mult),
                                reads=[("ps", pyb), ("t32", gi)], writes=[("t32", ti)])
                            S.add("dve", lambda e, ti=ti, n=n, e_=e_, t0=t0: e.tensor_tensor(
                                out=maT[:, e_, t0:t0 + n], in0=maT[:, e_, t0:t0 + n], in1=T32(ti)[:, 0:n], op=ALU.add),
                                reads=[("t32", ti)] + [("maT", e_, tb0 + jj) for jj in range(tn)],
                                writes=[("maT", e_, tb0 + jj) for jj in range(tn)])

            gen0 = None
            if pidx + 1 < len(passes):
                gen0 = phase0_steps(l, passes[pidx + 1][0], passes[pidx + 1][1], 0)
            elif l + 1 < NL:
                nB0, nNB = layer_passes[l + 1][0][0], layer_passes[l + 1][0][1]
                if nB0 + nNB <= B0 or nB0 >= B0 + NB:
                    gen0 = phase0_steps(l + 1, nB0, nNB, 1 if HALO_KV else 0)
                    hoisted_next_layer[l + 1] = True
                elif HALO_KV and (nB0 + 1 >= B0 + NB or nB0 + nNB <= B0):
                    gen0 = phase0_steps(l + 1, nB0, nNB, 1, "nonhalo")
                    hoisted_next_layer[l + 1] = "nonhalo"
            last_pass_of_all = (l == NL - 1 and pidx + 1 == len(passes) and NSLOT >= 4)
            if last_pass_of_all:
                wos = [load_w([(0, 256, wo_d[l, :, fp * 256:(fp + 1) * 256])]) for fp in range(4)]
                for (tb0, tn, hl) in tiles:
                    if hl:
                        continue
                    n = tn * 128
                    g0 = (B0 + tb0) * 128
                    for f in range(8):
                        pp = proj(wos[f // 2], f % 2, maT, "maT", tb0, tn)
                        S.add("dve", lambda e, pp=pp, n=n, f=f, g0=g0: e.tensor_tensor(
                            out=xT[:, f, g0:g0 + n], in0=psb[pp][:, 0:n], in1=xT[:, f, g0:g0 + n], op=ALU.add),
                            reads=[("ps", pp)] + [("xT", f, B0 + tb0 + jj) for jj in range(tn)],
                            writes=[("xT", f, B0 + tb0 + jj) for jj in range(tn)])
                    store_blocks([b_ for b_ in range(B0 + tb0, B0 + tb0 + tn) if b_ >= OUT0])
            else:
                for fp in range(4):
                    si = load_w([(0, 256, wo_d[l, :, fp * 256:(fp + 1) * 256])])
                    for ff in range(2):
                        f = fp * 2 + ff
                        for (tb0, tn, hl) in tiles:
                            if hl:
                                continue
                            n = tn * 128
                            g0 = (B0 + tb0) * 128
                            pp = proj(si, ff, maT, "maT", tb0, tn)
                            S.add("dve", lambda e, pp=pp, n=n, f=f, g0=g0: e.tensor_tensor(
                                out=xT[:, f, g0:g0 + n], in0=psb[pp][:, 0:n], in1=xT[:, f, g0:g0 + n], op=ALU.add),
                                reads=[("ps", pp)] + [("xT", f, B0 + tb0 + jj) for jj in range(tn)],
                                writes=[("xT", f, B0 + tb0 + jj) for jj in range(tn)])
                            gen0 = advance(gen0, 2)
                advance(gen0, 10 ** 6)

            if l == NL - 1 and pidx + 1 < len(passes):
                store_blocks([b_ for b_ in range(max(OUT0, B0 + halo), B0 + NB)])

    store_blocks([b_ for b_ in range(OUT0, NBLK) if ("out", b_ - OUT0) not in outs])
    S.add("sp", None, reads=outs)

    S.emit(nc, es)
    es.close()
    return nc


_CACHE = {}
FUSED = True


def _get_nc(key, *args):
    if key not in _CACHE:
        _CACHE[key] = build(*args)
    return _CACHE[key]


def kernel(x, norm_g, w_in, conv_w, q_norm_g, k_norm_g, sinks, w_conv_out, w_attn_out, gate_b, w_out):
    x = np.ascontiguousarray(np.asarray(x, dtype=np.float32))
    f = lambda a: np.ascontiguousarray(np.asarray(a, dtype=np.float32))
    norm_g, w_in, conv_w, q_norm_g, k_norm_g, sinks = map(f, (norm_g, w_in, conv_w, q_norm_g, k_norm_g, sinks))
    w_conv_out, w_attn_out, gate_b, w_out = map(f, (w_conv_out, w_attn_out, gate_b, w_out))
    B, SEQ, _ = x.shape
    L = norm_g.shape[0]
    HALF = SEQ // 2
    n = 8
    if not FUSED:
        HALO = 128
        NBLK = (HALF + HALO) // 128
        nb1 = (NBLK + 1) // 2
        passes = [(0, nb1), (nb1, NBLK - nb1)]
        nc = _get_nc(("unf", NBLK), 1, NBLK, 1, 1, [passes], 4)
        cur = x
        for l in range(L):
            in_maps = []
            for c in range(n):
                b, hf = c // 2, c % 2
                xs = np.zeros((HALF + HALO, D), np.float32)
                if hf == 0:
                    xs[HALO:] = cur[b, 0:HALF]
                else:
                    xs[:] = cur[b, HALF - HALO:SEQ]
                in_maps.append({
                    "x": xs, "w_in": w_in[l:l + 1], "w_co": w_conv_out[l:l + 1], "w_ao": w_attn_out[l:l + 1],
                    "w_o": w_out[l:l + 1],
                    "params": pack_params(norm_g[l:l + 1], gate_b[l:l + 1], conv_w[l:l + 1], q_norm_g[l:l + 1],
                                          k_norm_g[l:l + 1], sinks[l:l + 1], float(hf)),
                })
            res = run_bass_kernel_spmd(nc, in_maps, core_ids=list(range(n)))
            nxt = np.empty_like(cur)
            for c in range(n):
                b, hf = c // 2, c % 2
                nxt[b, hf * HALF:(hf + 1) * HALF] = res.results[c]["out"]
            cur = nxt
        return cur
    else:
        NBLK = HALF // 128
        nbA = NBLK // 2
        lp = [[(nbA - 1, NBLK - nbA + 1, "first+send"), (0, nbA, "remote")] for l in range(L)]
        nc = _get_nc(("fused", NBLK, L), L, NBLK, 0, 0, lp, 5, True, True, 2)
        in_maps = []
        for c in range(n):
            b, hf = c // 2, c % 2
            in_maps.append({
                "x": np.ascontiguousarray(x[b, hf * HALF:(hf + 1) * HALF]),
                "w_in": w_in, "w_co": w_conv_out, "w_ao": w_attn_out, "w_o": w_out,
                "params": pack_params(norm_g, gate_b, conv_w, q_norm_g, k_norm_g, sinks, float(hf)),
            })
        res = run_bass_kernel_spmd(nc, in_maps, core_ids=list(range(n)))
        out = np.empty_like(x)
        for c in range(n):
            b, hf = c // 2, c % 2
            out[b, hf * HALF:(hf + 1) * HALF] = res.results[c]["out"]
        return out
```
